# Optimizing a Trainium2 kernel written in Bass

```python
import jax, jax.numpy as jnp
from jax import lax
import numpy as np

D_MODEL = 2048
BATCH = 8
SEQ = 4096
DEPTH = 2
DEC_BATCH = 8
DEC_SEQ = 2048
PAST_LEN = 128

HEAD_DIM = 128
DIL_GROUPS = ((128, 1), (512, 4), (2048, 16))
N_GROUPS = len(DIL_GROUPS)
A_SLOTS = D_MODEL // 256
A_HEADS = N_GROUPS * A_SLOTS
RET_HEADS = D_MODEL // 256
RET_DK = D_MODEL // RET_HEADS
RET_DV = RET_DK
RET_CHUNK = 128
CROSS_HEADS = 4
CROSS_DIM = CROSS_HEADS * HEAD_DIM
N_MEM = 256
D_FF = -(-8 * D_MODEL // (3 * 256)) * 256
N_MIXERS = 2
N_A = (DEPTH + 1) // 2
N_B = DEPTH // 2
A_IN = 3 * A_HEADS * HEAD_DIM + CROSS_DIM
A_OUT = A_SLOTS * HEAD_DIM + CROSS_DIM
B_IN = 2 * RET_HEADS * RET_DK + 2 * RET_HEADS * RET_DV + CROSS_DIM
B_OUT = RET_HEADS * RET_DV + CROSS_DIM
EPS = 1e-6
F32 = jnp.float32

kernel_name = 'hybrid_dilated_retention_encoder'


def rmsnorm(x, g):
    xf = x.astype(F32)
    y = xf * lax.rsqrt(jnp.mean(xf * xf, axis=-1, keepdims=True) + EPS)
    return (y * g.astype(F32)).astype(x.dtype)


def alibi_slopes():
    j = jnp.arange(1, A_HEADS + 1, dtype=F32)
    return (2.0 ** (-8.0 * j / A_HEADS)).reshape(N_GROUPS, A_SLOTS)


def dilated_window_attention(q, k, v, window, dil, slopes):
    b, s, h, hd = q.shape
    w = window // (2 * dil)
    L = s // dil
    nb = -(-L // w)
    lp = nb * w

    def split(t):
        return t.reshape(b, L, dil, h, hd).transpose(0, 2, 3, 1, 4)

    qb = jnp.pad(split(q), ((0, 0), (0, 0), (0, 0), (0, lp - L), (0, 0))).reshape(b, dil, h, nb, w, hd)

    def band(t):
        t = jnp.pad(split(t), ((0, 0), (0, 0), (0, 0), (w, lp - L + w), (0, 0))).reshape(b, dil, h, nb + 2, w, hd)
        return jnp.concatenate([t[:, :, :, :-2], t[:, :, :, 1:-1], t[:, :, :, 2:]], axis=4)

    kb, vb = band(k), band(v)
    scores = jnp.einsum('brhnqd,brhnkd->brhnqk', qb, kb, preferred_element_type=F32) * (hd ** -0.5)
    il = jnp.arange(w)[:, None]
    c = jnp.arange(3 * w)[None, :]
    rel = c - w - il
    jidx = (jnp.arange(nb) * w)[:, None, None] + c[None] - w
    valid = (jnp.abs(rel) <= w)[None] & (jidx >= 0) & (jidx < L)
    bias = -slopes.astype(F32)[:, None, None, None] * (dil * jnp.abs(rel)).astype(F32)[None, None]
    scores = jnp.where(valid, scores + bias, -jnp.inf)
    lse = jax.nn.logsumexp(scores, axis=-1)
    p = jnp.exp(scores - lse[..., None])
    o = jnp.einsum('brhnqk,brhnkd->brhnqd', p.astype(v.dtype), vb)
    o = o.reshape(b, dil, h, lp, hd)[:, :, :, :L].transpose(0, 3, 1, 2, 4).reshape(b, s, h, hd)
    lse = lse.reshape(b, dil, h, lp)[..., :L].transpose(0, 3, 1, 2).reshape(b, s, h)
    return o, lse


def mixer_a(xn, w_in):
    b, s, _ = xn.shape
    proj = xn @ w_in
    nqkv = 3 * A_HEADS * HEAD_DIM
    qkv = proj[..., :nqkv].reshape(b, s, 3, N_GROUPS, A_SLOTS, HEAD_DIM)
    qc = proj[..., nqkv:]
    slopes = alibi_slopes()
    outs, lses = [], []
    for g, (win, dil) in enumerate(DIL_GROUPS):
        o, l = dilated_window_attention(qkv[:, :, 0, g], qkv[:, :, 1, g], qkv[:, :, 2, g], win, dil, slopes[g])
        outs.append(o)
        lses.append(l)
    wts = jax.nn.softmax(jnp.stack(lses, axis=0), axis=0)
    o = jnp.einsum('gbsh,gbshd->bshd', wts, jnp.stack(outs, axis=0).astype(F32))
    return o.reshape(b, s, A_SLOTS * HEAD_DIM).astype(xn.dtype), qc


def retention_direction(q, k, v, log_gamma, reverse):
    n, b, h, c, dk = q.shape
    dv = v.shape[-1]
    i = jnp.arange(c, dtype=F32)
    diff = i[:, None] - i[None, :]
    lg = log_gamma[:, None]
    if reverse:
        mask = diff < 0
        dist = -diff
        q_dec = jnp.exp((c - i)[None, :] * lg)
        k_dec = jnp.exp(i[None, :] * lg)
    else:
        mask = diff >= 0
        dist = diff
        q_dec = jnp.exp((i + 1.0)[None, :] * lg)
        k_dec = jnp.exp((c - 1.0 - i)[None, :] * lg)
    intra = jnp.where(mask[None], jnp.exp(jnp.where(mask, dist, 0.0)[None] * lg[:, :, None]), 0.0)
    chunk_dec = jnp.exp(c * log_gamma)[None, :, None, None]

    def step(state, qkv):
        qc, kc, vc = qkv
        sc = jnp.einsum('bhid,bhjd->bhij', qc, kc) * intra
        out = jnp.einsum('bhij,bhje->bhie', sc, vc) + jnp.einsum('bhid,bhde->bhie', qc * q_dec[:, :, None], state)
        state = chunk_dec * state + jnp.einsum('bhjd,bhje->bhde', kc * k_dec[:, :, None], vc)
        return state, out

    init = jnp.zeros((b, h, dk, dv), F32)
    _, out = lax.scan(step, init, (q, k, v), reverse=reverse)
    return out


def mixer_b(xn, w_in, e_fwd, e_bwd):
    b, s, _ = xn.shape
    n = s // RET_CHUNK
    proj = xn @ w_in
    hk, hv = RET_HEADS * RET_DK, RET_HEADS * RET_DV
    q = proj[..., :hk]
    k = proj[..., hk:2 * hk]
    v = proj[..., 2 * hk:2 * hk + hv]
    gate = proj[..., 2 * hk + hv:2 * hk + 2 * hv]
    qc = proj[..., 2 * hk + 2 * hv:]

    def chunks(t, d):
        return t.astype(F32).reshape(b, n, RET_CHUNK, RET_HEADS, d).transpose(1, 0, 3, 2, 4)

    qh = chunks(q, RET_DK) * (RET_DK ** -0.5)
    kh, vh = chunks(k, RET_DK), chunks(v, RET_DV)
    lg_f = jnp.log1p(-jnp.exp2(-e_fwd.astype(F32)))
    lg_b = jnp.log1p(-jnp.exp2(-e_bwd.astype(F32)))
    y = retention_direction(qh, kh, vh, lg_f, False) + retention_direction(qh, kh, vh, lg_b, True)
    y = y.transpose(1, 0, 3, 2, 4).reshape(b, s, RET_HEADS, RET_DV)
    y = y * lax.rsqrt(jnp.mean(y * y, axis=-1, keepdims=True) + EPS)
    y = y.reshape(b, s, hv) * jax.nn.silu(gate.astype(F32))
    return y.astype(xn.dtype), qc


def memory_cross_attention(qc, mem, g_mem, w_mem_kv):
    b, s, _ = qc.shape
    m = rmsnorm(mem, g_mem)
    kv = (m @ w_mem_kv).reshape(b, mem.shape[1], 2, CROSS_HEADS, HEAD_DIM)
    q = qc.reshape(b, s, CROSS_HEADS, HEAD_DIM)
    sc = jnp.einsum('bshd,bmhd->bhsm', q, kv[:, :, 0], preferred_element_type=F32) * (HEAD_DIM ** -0.5)
    p = jax.nn.softmax(sc, axis=-1)
    o = jnp.einsum('bhsm,bmhd->bshd', p.astype(mem.dtype), kv[:, :, 1])
    return o.reshape(b, s, CROSS_DIM)


def trunk(x, mem, norm_mix, norm_mem, w_mem_kv, a_w_in, a_w_out, b_w_in, b_w_out, b_decay_fwd, b_decay_bwd,
          norm_ffn, w_gate_up, w_down, norm_final):
    for i in range(DEPTH):
        xn = rmsnorm(x, norm_mix[i])
        j = i // N_MIXERS
        if i % N_MIXERS == 0:
            mix, qc = mixer_a(xn, a_w_in[j])
            w_out = a_w_out[j]
        else:
            mix, qc = mixer_b(xn, b_w_in[j], b_decay_fwd[j], b_decay_bwd[j])
            w_out = b_w_out[j]
        cross = memory_cross_attention(qc, mem, norm_mem[i], w_mem_kv[i])
        x = x + jnp.concatenate([mix, cross], axis=-1) @ w_out
        hdn = rmsnorm(x, norm_ffn[i]) @ w_gate_up[i]
        gate, up = jnp.split(hdn, 2, axis=-1)
        x = x + (jax.nn.silu(gate) * up) @ w_down[i]
    return rmsnorm(x, norm_final)


def setup_inputs(seed: int = 0) -> dict:
    key = jax.random.key(seed)
    ks = jax.random.split(key, 20)

    def dense(k, shape, fan_in):
        return jax.random.normal(k, shape, F32) * (fan_in ** -0.5)

    def gain(k, shape):
        return 1.0 + 0.02 * jax.random.normal(k, shape, F32)

    base = 5.0 + jnp.arange(RET_HEADS, dtype=F32)
    return {
        'x_prompt': jax.random.normal(ks[0], (BATCH, SEQ, D_MODEL), F32),
        'x_sample': jax.random.normal(ks[1], (DEC_BATCH, DEC_SEQ, D_MODEL), F32),
        'mem_prompt': jax.random.normal(ks[2], (BATCH, N_MEM, D_MODEL), F32),
        'mem_sample': jax.random.normal(ks[3], (DEC_BATCH, N_MEM, D_MODEL), F32),
        'norm_mix': gain(ks[4], (DEPTH, D_MODEL)),
        'norm_mem': gain(ks[5], (DEPTH, D_MODEL)),
        'w_mem_kv': dense(ks[6], (DEPTH, D_MODEL, 2 * CROSS_DIM), D_MODEL),
        'a_w_in': dense(ks[7], (N_A, D_MODEL, A_IN), D_MODEL),
        'a_w_out': dense(ks[8], (N_A, A_OUT, D_MODEL), A_OUT),
        'b_w_in': dense(ks[9], (N_B, D_MODEL, B_IN), D_MODEL),
        'b_w_out': dense(ks[10], (N_B, B_OUT, D_MODEL), B_OUT),
        'b_decay_fwd': base[None] + 0.1 * jax.random.normal(ks[11], (N_B, RET_HEADS), F32),
        'b_decay_bwd': base[None] + 0.5 + 0.1 * jax.random.normal(ks[12], (N_B, RET_HEADS), F32),
        'norm_ffn': gain(ks[13], (DEPTH, D_MODEL)),
        'w_gate_up': dense(ks[14], (DEPTH, D_MODEL, 2 * D_FF), D_MODEL),
        'w_down': dense(ks[15], (DEPTH, D_FF, D_MODEL), D_FF),
        'norm_final': gain(ks[16], (D_MODEL,)),
    }


def reference(x_prompt, x_sample, mem_prompt, mem_sample, norm_mix, norm_mem, w_mem_kv, a_w_in, a_w_out,
              b_w_in, b_w_out, b_decay_fwd, b_decay_bwd, norm_ffn, w_gate_up, w_down, norm_final):
    y_prompt = trunk(x_prompt, mem_prompt, norm_mix, norm_mem, w_mem_kv, a_w_in, a_w_out, b_w_in, b_w_out,
                     b_decay_fwd, b_decay_bwd, norm_ffn, w_gate_up, w_down, norm_final)
    y_sample = trunk(x_sample, mem_sample, norm_mix, norm_mem, w_mem_kv, a_w_in, a_w_out, b_w_in, b_w_out,
                     b_decay_fwd, b_decay_bwd, norm_ffn, w_gate_up, w_down, norm_final)
    return (y_prompt, y_sample)
```

```python
import numpy as np
import concourse.bass as bass
import concourse.mybir as mybir
from concourse.bass_utils import run_bass_kernel_spmd

F32 = mybir.dt.float32
BF16 = mybir.dt.bfloat16
U8 = mybir.dt.uint8
AF = mybir.ActivationFunctionType
ALU = mybir.AluOpType
AX = mybir.AxisListType

D = 2048
KC = 16
HD = 128
DFF = 5632
NFF = 44
A_IN = 9728
B_IN = 8704
EPS = 1e-6
DIL = ((128, 1), (512, 4), (2048, 16))
TT = 512
ARENA = 200 * 1024
PARANOID = True


class Ev:
    def __init__(self, sem, name):
        self.sem = sem
        self.count = 0
        self.name = name


class Res:
    __slots__ = ("name", "w", "r", "ev", "excl")

    def __init__(self, name, excl=False):
        self.name = name
        self.w = {}
        self.r = {}
        self.ev = None
        self.excl = excl


class Tile:
    def __init__(self, ap, res):
        self.ap = ap
        self.res = res

    def __getitem__(self, k):
        return self.ap[k]


class Eng:
    def __init__(self, name, ev):
        self.name = name
        self.ev = ev
        self.seen = {}
        self.q = []


class K:
    def __init__(self, nc, sems):
        self.nc = nc
        self.sems = list(sems)
        self.eng = {}
        for n in ("pe", "act", "dve", "pool", "sp"):
            self.eng[n] = Eng(n, Ev(self.sems.pop(), n))
        self.dma_evs = []
        self.free_evs = []
        self.used_evs = []
        self.bg = set()
        self.nops = 0

    def new_ev(self, name):
        if self.free_evs:
            ev = self.free_evs.pop()
        else:
            ev = Ev(self.sems.pop(), name)
            self.dma_evs.append(ev)
        self.used_evs.append(ev)
        return ev

    def _waits(self, e, reads, writes, skip=None):
        deps = {}
        for t in reads:
            r = t.res if isinstance(t, Tile) else t
            for ev, v in r.w.items():
                if deps.get(ev, 0) < v:
                    deps[ev] = v
        for t in writes:
            r = t.res if isinstance(t, Tile) else t
            for ev, v in r.w.items():
                if ev is skip:
                    continue
                if deps.get(ev, 0) < v:
                    deps[ev] = v
            for ev, v in r.r.items():
                if deps.get(ev, 0) < v:
                    deps[ev] = v
        for ev, v in deps.items():
            if ev is e.ev and (e.name == "pe" or not PARANOID):
                continue
            if e.seen.get(ev, 0) >= v:
                continue
            e.seen[ev] = v
            e.q.append(("wait", ev.sem, v))

    def _mark(self, ev, val, reads, writes):
        for t in reads:
            r = t.res if isinstance(t, Tile) else t
            r.r[ev] = val
        for t in writes:
            r = t.res if isinstance(t, Tile) else t
            r.w = {ev: val}
            r.r = {}

    def op(self, en, fn, reads=(), writes=()):
        e = self.eng[en]
        ex = [t for t in reads if (t.res if isinstance(t, Tile) else t).excl]
        if ex:
            reads = [t for t in reads if not (t.res if isinstance(t, Tile) else t).excl]
            writes = list(writes) + ex
        self._waits(e, reads, writes)
        e.ev.count += 1
        e.q.append(("op", fn, e.ev.sem))
        self._mark(e.ev, e.ev.count, reads, writes)
        self.nops += 1

    def dma(self, qn, out_ap, in_ap, ev, reads=(), writes=()):
        e = self.eng[qn]
        self._waits(e, reads, writes, skip=ev)
        ev.count += 16
        e.q.append(("dma", out_ap, in_ap, ev.sem))
        self._mark(ev, ev.count, reads, writes)

    def barrier(self):
        sp = self.eng["sp"]
        for ev in self.dma_evs:
            if ev in self.bg:
                continue
            if ev.count > sp.seen.get(ev, 0):
                sp.seen[ev] = ev.count
                sp.q.append(("wait", ev.sem, ev.count))
        sp.ev.count += 1
        sp.q.append(("inc", sp.ev.sem))
        for e in self.eng.values():
            for f in self.eng.values():
                if f is e:
                    continue
                if f.ev.count > e.seen.get(f.ev, 0):
                    e.seen[f.ev] = f.ev.count
                    e.q.append(("wait", f.ev.sem, f.ev.count))
        self.free_evs.extend(ev for ev in self.used_evs if ev not in self.bg)
        self.used_evs = [ev for ev in self.used_evs if ev in self.bg]

    def replay(self, name, h):
        for it in self.eng[name].q:
            k = it[0]
            if k == "wait":
                h.wait_ge(it[1], it[2])
            elif k == "op":
                ins = it[1](h)
                ins.then_inc(it[2], 1)
            elif k == "dma":
                h.dma_start(out=it[1], in_=it[2]).then_inc(it[3], 16)
            elif k == "inc":
                h.sem_inc(it[1], 1)


class Arena:
    def __init__(self, base_ap, size):
        self.base = base_ap
        self.size = size
        self.off = 0
        self.mark_ = 0

    def alloc(self, name, free_shape, dtype, ev=None):
        esz = {F32: 4, BF16: 2, U8: 1}[dtype]
        n = 1
        for s in free_shape:
            n *= s
        nb = (n * esz + 63) // 64 * 64
        assert self.off + nb <= self.size, f"arena overflow at {name}: {self.off}+{nb}"
        ap = self.base[:, self.off:self.off + n * esz]
        if dtype != U8:
            ap = ap.bitcast(dtype)
        if len(free_shape) == 2:
            ap = ap.rearrange("p (a b) -> p a b", b=free_shape[1])
        elif len(free_shape) == 3:
            ap = ap.rearrange("p (a b c) -> p a b c", b=free_shape[1], c=free_shape[2])
        elif len(free_shape) == 4:
            ap = ap.rearrange("p (a b c d) -> p a b c d", b=free_shape[1], c=free_shape[2], d=free_shape[3])
        self.off += nb
        r = Res(name)
        r.ev = ev
        return Tile(ap, r)

    def mark(self):
        self.mark_ = self.off

    def reset(self):
        self.off = self.mark_


class Stream:
    def __init__(self):
        self.units = []

    def add(self, compute, load=None, pool=None, release=None):
        if release is None:
            release = [pool] if load is not None else []
        self.units.append((compute, load, pool, release))

    def run(self, pools, lookahead=8, hook=None):
        loads = [(i, u) for i, u in enumerate(self.units) if u[1] is not None]
        issued = {p: 0 for p in pools}
        consumed = {p: 0 for p in pools}
        slot_of = {}
        li = 0
        for i, (compute, load, pool, release) in enumerate(self.units):
            while li < len(loads):
                j, (c2, l2, p2, r2) = loads[li]
                if j > i + lookahead:
                    break
                if issued[p2] - consumed[p2] >= len(pools[p2]):
                    assert j != i, "stream deadlock"
                    break
                slot = pools[p2][issued[p2] % len(pools[p2])]
                issued[p2] += 1
                slot_of[j] = slot
                l2(slot)
                li += 1
            compute(slot_of.get(i))
            if hook is not None:
                hook()
            for p in release:
                consumed[p] += 1


def _mmg(k, items, reads, writes):
    items = list(items)

    def fn(h):
        ins = None
        for (o, l, r, s, e) in items:
            ins = h.matmul(o, l, r, start=s, stop=e)
        return ins
    k.op("pe", fn, reads, writes)


def _copy(k, en, out, in_, reads, writes):
    if en == "act":
        k.op("act", lambda h: h.activation(out, in_, AF.Copy), reads, writes)
    else:
        k.op(en, lambda h: h.tensor_copy(out, in_), reads, writes)


class Ctx:
    pass


def build(S_list):
    from contextlib import ExitStack
    nc = bass.Bass("TRN2", target_bir_lowering=False)
    c = Ctx()
    c.nc = nc

    def din(name, shape):
        return nc.dram_tensor(name, shape, F32, kind="ExternalInput").ap()

    import os
    _dbg = os.environ.get("KDBG_OUT", "").split(",")

    def dscr(name, shape, dt=BF16):
        return nc.dram_tensor(name, shape, dt, kind=("ExternalOutput" if name in _dbg else "Internal")).ap()

    SMAX = max(S_list)
    c.x_in = [din(f"x{i}", [S, D]) for i, S in enumerate(S_list)]
    c.mem_in = [din(f"mem{i}", [256, D]) for i, S in enumerate(S_list)]
    c.y_out = [nc.dram_tensor(f"y{i}", [S, D], F32, kind="ExternalOutput").ap() for i, S in enumerate(S_list)]
    c.gcols = din("gcols", [7, 128, 16])
    c.gfinal = din("gfinal", [1, D])
    c.decay = din("decay", [1, 16])
    c.ident = din("ident", [128, 128])
    c.etab = din("etab", [24, 128, 8, 128])
    c.rtab = din("rtab", [128, 4, 128])
    c.ccol = din("ccol", [128, 4])
    w_mem_kv = din("w_mem_kv", [2, D, 1024])
    a_w_in = din("a_w_in", [D, A_IN])
    a_w_out = din("a_w_out", [1536, D])
    b_w_in = din("b_w_in", [D, B_IN])
    b_w_out = din("b_w_out", [2560, D])
    w_gate_up = din("w_gate_up", [2, D, 2 * DFF])
    w_down = din("w_down", [2, DFF, D])
    c.AWF = dscr("AWF", [52, 128, KC, 128])
    c.AWT = dscr("AWT", [6, 128, KC, 512])
    c.BWF = dscr("BWF", [36, 128, KC, 128])
    c.BWT = dscr("BWT", [8, 128, KC, 512])
    c.GU = [dscr(f"GU{l}", [88, 128, KC, 128]) for l in range(2)]
    c.WD = [dscr(f"WD{l}", [DFF, D]) for l in range(2)]
    c.AWO = dscr("AWO", [1536, D])
    c.BWO = dscr("BWO", [2560, D])
    c.MKF = [dscr(f"MKF{l}", [4, 128, KC, 128]) for l in range(2)]
    c.MKT = [dscr(f"MKT{l}", [128, KC, 512]) for l in range(2)]
    c.QKT = dscr("QKT", [48, 128, SMAX])
    c.VS = dscr("VS", [SMAX, 3072])
    c.SG = dscr("SG", [SMAX, 2048])
    c.MIXT = dscr("MIXT", [20, 128, SMAX])
    c.X1 = dscr("X1", [SMAX, D], F32)

    with ExitStack() as es:
        arena_t = es.enter_context(nc.sbuf_tensor("arena", [128, ARENA], U8))
        ps_t = es.enter_context(nc.psum_tensor("ps", [128, 4096], F32))
        sems = [es.enter_context(nc.semaphore(f"s{i}")) for i in range(48)]
        k = K(nc, sems)
        c.k = k
        ar = Arena(arena_t, ARENA)
        c.ar = ar
        c.bank = [Tile(ps_t[:, b * 512:(b + 1) * 512], Res(f"bank{b}", excl=True)) for b in range(8)]

        pev = k.new_ev("prologue")

        def cast_fm(dst, dst0, src, col0, nblk):
            for b in range(nblk):
                s = src[:, col0 + b * 128: col0 + (b + 1) * 128].rearrange("(kc p) c -> p kc c", p=128)
                k.dma("pool", dst[dst0 + b], s, pev)

        def cast_tm(dst_ap, src, col0):
            s = src[:, col0: col0 + 512].rearrange("(kc p) c -> p kc c", p=128)
            k.dma("pool", dst_ap, s, pev)

        def cast_nat(dst, src, rows):
            step = 512
            for r0 in range(0, rows, step):
                r1 = min(rows, r0 + step)
                k.dma("pool", dst[r0:r1, :], src[r0:r1, :], pev)

        for l in range(2):
            cast_fm(c.MKF[l], 0, w_mem_kv[l], 0, 4)
            cast_tm(c.MKT[l], w_mem_kv[l], 512)
        cast_fm(c.AWF, 0, a_w_in, 0, 24)
        cast_fm(c.AWF, 24, a_w_in, 3072, 24)
        cast_fm(c.AWF, 48, a_w_in, 9216, 4)
        for b in range(6):
            cast_tm(c.AWT[b], a_w_in, 6144 + b * 512)
        def prologue_part2():
          pev = k.new_ev("prologue2")
          k.bg.add(pev)
          c.bg_ev = pev
          c.bg_casts = []

          def cast_fm(dst, dst0, src, col0, nblk):
            for b in range(nblk):
                s_ = src[:, col0 + b * 128: col0 + (b + 1) * 128].rearrange("(kc p) c -> p kc c", p=128)
                c.bg_casts.append((dst[dst0 + b], s_))

          def cast_tm(dst_ap, src, col0):
            s_ = src[:, col0: col0 + 512].rearrange("(kc p) c -> p kc c", p=128)
            c.bg_casts.append((dst_ap, s_))

          def cast_nat(dst, src, rows):
            step = 512
            for r0 in range(0, rows, step):
                r1 = min(rows, r0 + step)
                c.bg_casts.append((dst[r0:r1, :], src[r0:r1, :]))
          cast_nat(c.AWO, a_w_out, 1536)
          for l in range(2):
            cast_fm(c.GU[l], 0, w_gate_up[l], 0, 88)
            cast_nat(c.WD[l], w_down[l], DFF)
          cast_fm(c.BWF, 0, b_w_in, 0, 16)
          cast_fm(c.BWF, 16, b_w_in, 2048, 16)
          cast_fm(c.BWF, 32, b_w_in, 8192, 4)
          for b in range(8):
            cast_tm(c.BWT[b], b_w_in, 4096 + b * 512)
          cast_nat(c.BWO, b_w_out, 2560)

        _stop = int(os.environ.get('KDBG_STOP', '99'))
        lev = k.new_ev("constload")
        identf = ar.alloc("identf", [128], F32)
        c.identb = ar.alloc("identb", [128], BF16)
        c.ones = ar.alloc("ones", [128], BF16)
        c.gcol = ar.alloc("gcol", [7, KC], F32)
        c.rt = ar.alloc("rtab", [4, 128], F32)
        c.cc = ar.alloc("ccol", [4], F32)
        c.dec = ar.alloc("dec", [16], F32)
        c.lg = ar.alloc("lg", [16], F32)
        c.kd = ar.alloc("kdcols", [4, 8], F32)
        c.gC = ar.alloc("gC", [16], F32)
        c.zero = ar.alloc("zero", [1], F32)
        k.dma("sp", identf.ap, c.ident, lev, writes=[identf])
        k.dma("sp", c.gcol.ap, c.gcols.rearrange("g p k -> p g k"), k.new_ev("c1"), writes=[c.gcol])
        k.dma("sp", c.rt.ap, c.rtab, k.new_ev("c2"), writes=[c.rt])
        k.dma("sp", c.cc.ap, c.ccol, k.new_ev("c3"), writes=[c.cc])
        k.dma("sp", c.dec.ap, c.decay.partition_broadcast(128), k.new_ev("c4"), writes=[c.dec])
        if _stop == -3:
            S_list = []
        k.op("dve", lambda h: h.tensor_copy(c.identb.ap, identf.ap), [identf], [c.identb])
        k.op("dve", lambda h: h.memset(c.ones.ap, 1.0), [], [c.ones])
        k.op("dve", lambda h: h.memset(c.zero.ap, 0.0), [], [c.zero])
        if _stop == -2:
            S_list = []
        k.op("act", lambda h: h.activation(c.lg.ap, c.dec.ap, AF.Exp, scale=-float(np.log(2.0))), [c.dec], [c.lg])
        k.op("act", lambda h: h.activation(c.lg.ap, c.lg.ap, AF.Ln, scale=-1.0, bias=1.0), [c.lg], [c.lg])
        rscale = float(256 ** -0.5)
        for h_ in range(8):
            for d_, (tabi, outi, mul) in enumerate([(0, 0, 1.0), (1, 1, 1.0), (2, 2, rscale), (3, 3, rscale)]):
                lgc = c.lg[:, (h_ if d_ in (0, 2) else 8 + h_):(h_ if d_ in (0, 2) else 8 + h_) + 1]
                o = c.kd[:, outi, h_:h_ + 1]
                i_ = c.cc[:, tabi:tabi + 1]
                k.op("act", (lambda o=o, i_=i_, lgc=lgc: (lambda h: h.activation(o, i_, AF.Exp, scale=lgc)))(), [c.cc, c.lg], [c.kd])
        k.op("dve", lambda h: h.tensor_scalar(c.kd[:, 2:4, :], c.kd[:, 2:4, :], rscale, 0.0, ALU.mult, ALU.add), [c.kd], [c.kd])
        k.op("act", lambda h: h.activation(c.gC.ap, c.lg.ap, AF.Exp, scale=128.0), [c.lg], [c.gC])
        c.kmemT = [ar.alloc(f"kmemT{l}", [4, 256], BF16) for l in range(2)]
        c.vmem = [ar.alloc(f"vmem{l}", [2, 512], BF16) for l in range(2)]
        ar.mark()
        k.barrier()
        prologue_part2()

        if _stop == -1:
            S_list = []
        for si, S in enumerate(S_list):
            run_sequence(c, si, S)

        k.barrier()
        c.stats = {n: len(e.q) for n, e in k.eng.items()}
        nc._kstats = c.stats
        with nc.Block() as block:
            @block.tensor
            def _(h):
                k.replay("pe", h)

            @block.scalar
            def _(h):
                k.replay("act", h)

            @block.vector
            def _(h):
                k.replay("dve", h)

            @block.gpsimd
            def _(h):
                k.replay("pool", h)

            @block.sync
            def _(h):
                k.replay("sp", h)
    return nc


def norm_transpose(c, xt, gidx, xnT, st, nsub=4, width=TT):
    k = c.k
    k.op("dve", lambda h: h.memset(st.ss[:, 0:nsub], 0.0), [], [st.ss])
    for sub in range(nsub):
        ssc = st.ss[:, sub:sub + 1]
        k.op("act", (lambda ssc=ssc, sub=sub: lambda h: h.activation(st.junk.ap, xt[:, sub, :], AF.Square, accum_out=ssc))(),
             [xt, st.ss], [st.junk, st.ss])
    k.op("act", lambda h: h.activation(st.rstd[:, 0:nsub], st.ss[:, 0:nsub], AF.Ln, scale=1.0 / D, bias=EPS), [st.ss], [st.rstd])
    k.op("act", lambda h: h.activation(st.rstd[:, 0:nsub], st.rstd[:, 0:nsub], AF.Exp, scale=-0.5), [st.rstd], [st.rstd])
    for sub in range(nsub):
        rsc = st.rstd[:, sub:sub + 1]
        xs = st.xs[sub % 2]
        k.op("dve", (lambda xs=xs, rsc=rsc, sub=sub: lambda h: h.tensor_scalar(xs.ap, xt[:, sub, :], rsc, 0.0, ALU.mult, ALU.add))(),
             [xt, st.rstd], [xs])
        for q4 in range(4):
            bk = c.bank[st.tbanks[(sub * 4 + q4) % len(st.tbanks)]]
            _mmg(k, [(bk[:, j * 128:(j + 1) * 128], xs[:, (q4 * 4 + j) * 128:(q4 * 4 + j + 1) * 128], c.identb.ap, True, True)
                     for j in range(4)], [xs, c.identb], [bk])
            for j in range(4):
                kc = q4 * 4 + j
                o = xnT[:, kc, sub * 128:(sub + 1) * 128]
                i_ = bk[:, j * 128:(j + 1) * 128]
                gs = c.gcol[:, gidx, kc:kc + 1]
                if q4 % 2 == 0:
                    k.op("act", (lambda o=o, i_=i_, gs=gs: lambda h: h.activation(o, i_, AF.Copy, scale=gs))(), [bk, c.gcol], [xnT])
                else:
                    k.op("dve", (lambda o=o, i_=i_, gs=gs: lambda h: h.tensor_scalar(o, i_, gs, 0.0, ALU.mult, ALU.add))(), [bk, c.gcol], [xnT])


def fm_block_mm(c, bk, w_ap, xnT, width=TT):
    _mmg(c.k, [(bk[:, 0:width], w_ap[:, kc, :], xnT[:, kc, 0:width], kc == 0, kc == KC - 1) for kc in range(KC)],
         [xnT, c.cur_wres], [bk])


def cross_attn(c, l, hc, qcT, outT, st, width=TT):
    k = c.k
    sc = float(HD ** -0.5)
    b_s = [c.bank[st.cbanks[0]], c.bank[st.cbanks[1]]]
    b_n = c.bank[st.cbanks[2]]
    b_d = c.bank[st.cbanks[3]]
    for kt in range(2):
        _mmg(k, [(b_s[kt][:, 0:width], c.kmemT[l][:, hc, kt * 128:(kt + 1) * 128], qcT[:, 0:width], True, True)], [c.kmemT[l], qcT], [b_s[kt]])
        o = st.pT[:, kt, 0:width]
        i_ = b_s[kt][:, 0:width]
        k.op("act", (lambda o=o, i_=i_: lambda h: h.activation(o, i_, AF.Exp, scale=sc))(), [b_s[kt]], [st.pT])
    _mmg(k, [(b_n[:, 0:width], c.vmem[l][:, kt, hc * 128:(hc + 1) * 128], st.pT[:, kt, 0:width], kt == 0, kt == 1) for kt in range(2)],
         [c.vmem[l], st.pT], [b_n])
    _mmg(k, [(b_d[:, 0:width], c.ones.ap, st.pT[:, kt, 0:width], kt == 0, kt == 1) for kt in range(2)], [c.ones, st.pT], [b_d])
    k.op("dve", lambda h: h.reciprocal(st.rden[:, 0:width], b_d[:, 0:width]), [b_d], [st.rden])
    k.op("dve", lambda h: h.tensor_tensor(outT[:, 0:width], b_n[:, 0:width], st.rden[:, 0:width], ALU.mult), [b_n, st.rden], [outT])


def bg_issue(c, n):
    lst = getattr(c, "bg_casts", None)
    while lst and n > 0:
        dst, src = lst.pop(0)
        c.k.dma("pool", dst, src, c.bg_ev)
        n -= 1


def memkv(c, si):
    k, ar = c.k, c.ar
    ar.reset()
    ev = k.new_ev("memload")
    mt = ar.alloc("mem", [2, D], F32)
    mT = ar.alloc("mT", [KC, 256], BF16)
    wf = ar.alloc("wf", [4, KC, 128], BF16, ev=k.new_ev("wf"))
    wt = ar.alloc("wt", [KC, 512], BF16, ev=k.new_ev("wt"))
    st = Ctx()
    st.ss = ar.alloc("ss", [4], F32)
    st.rstd = ar.alloc("rstd", [4], F32)
    st.junk = ar.alloc("junk", [D], BF16)
    st.xs = [ar.alloc(f"xs{i}", [D], BF16) for i in range(2)]
    st.tbanks = [0, 1]
    k.dma("sp", mt.ap, c.mem_in[si].rearrange("(s p) d -> p s d", p=128), ev, writes=[mt])
    for l in range(2):
        norm_transpose(c, mt, 2 + l, mT, st, nsub=2, width=256)
        k.dma("sp", wf.ap, c.MKF[l].rearrange("b p k c -> p b k c"), wf.res.ev, writes=[wf])
        k.dma("sp", wt.ap, c.MKT[l], wt.res.ev, writes=[wt])
        for hc in range(4):
            bk = c.bank[2 + hc % 2]
            _mmg(k, [(bk[:, 0:256], wf[:, hc, kc, :], mT[:, kc, :], kc == 0, kc == KC - 1) for kc in range(KC)], [wf, mT], [bk])
            o = c.kmemT[l][:, hc, :]
            _copy(k, "act", o, bk[:, 0:256], [bk], [c.kmemT[l]])
        for kt in range(2):
            bk = c.bank[4 + kt]
            _mmg(k, [(bk[:, 0:512], mT[:, kc, kt * 128:(kt + 1) * 128], wt[:, kc, :], kc == 0, kc == KC - 1) for kc in range(KC)], [wt, mT], [bk])
            o = c.vmem[l][:, kt, :]
            _copy(k, "dve", o, bk[:, 0:512], [bk], [c.vmem[l]])
    k.barrier()


def alloc_common(c, st, nbig=True):
    ar = c.ar
    st.ss = ar.alloc("ss", [4], F32)
    st.rstd = ar.alloc("rstd", [4], F32)
    st.junk = ar.alloc("junk", [D], BF16)
    st.xs = [ar.alloc(f"xs{i}", [D], BF16) for i in range(2)]
    st.pT = ar.alloc("pT", [2, TT], BF16)
    st.rden = ar.alloc("rden", [TT], F32)
    st.qcT = [ar.alloc(f"qcT{i}", [TT], BF16) for i in range(2)]
    st.crossT = [ar.alloc(f"crossT{i}", [TT], BF16, ev=c.k.new_ev("crossT")) for i in range(2)]
    st.ostage = [ar.alloc(f"ost{i}", [TT], BF16, ev=c.k.new_ev("ostage")) for i in range(4)]
    st.vstage = [ar.alloc(f"vst{i}", [4, TT], BF16, ev=c.k.new_ev("vstage")) for i in range(2)]
    st.oi = 0
    st.vi = 0
    st.ci = 0
    st.ei = 0


def in_proj_units(c, stream, l, t, S, xnT, st):
    k = c.k
    WF = c.AWF if l == 0 else c.BWF
    WT = c.AWT if l == 0 else c.BWT
    nqk = 48 if l == 0 else 32
    cross_base = 8 if l == 0 else 16
    cols = slice(t * TT, (t + 1) * TT)

    def ld_fm(b0):
        def f(slot):
            k.dma("sp", slot[:, 0:4], WF[b0:b0 + 4].rearrange("b p k c -> p b k c"), slot.res.ev, writes=[slot])
        return f

    def evac(bk, o):
        st.ei += 1
        _copy(k, "act" if st.ei % 2 else "dve", o, bk[:, 0:TT], [bk], [st.cur_o])

    def comp_qk(b0):
        def f(slot):
            c.cur_wres = slot
            for j in range(4):
                bk = c.bank[st.mbanks[st.oi % len(st.mbanks)]]
                og = st.ostage[st.oi % 4]
                st.oi += 1
                fm_block_mm(c, bk, slot[:, j], xnT)
                st.cur_o = og
                evac(bk, og.ap)
                k.dma("pool", c.QKT[b0 + j][:, cols], og.ap, og.res.ev, reads=[og])
        return f

    def comp_qc(slot):
        c.cur_wres = slot
        for hc in range(4):
            bk = c.bank[st.mbanks[st.oi % len(st.mbanks)]]
            st.oi += 1
            qc = st.qcT[st.ci % 2]
            ct = st.crossT[st.ci % 2]
            st.ci += 1
            fm_block_mm(c, bk, slot[:, hc], xnT)
            st.cur_o = qc
            evac(bk, qc.ap)
            cross_attn(c, l, hc, qc, ct, st)
            k.dma("pool", c.MIXT[cross_base + hc][:, cols], ct.ap, ct.res.ev, reads=[ct])

    for b0 in range(0, nqk, 4):
        stream.add(comp_qk(b0), ld_fm(b0), "w")
    stream.add(comp_qc, ld_fm(nqk), "w")

    ntm = 6 if l == 0 else 8

    def ld_tm(b):
        def f(slot):
            k.dma("sp", slot.ap.rearrange("p a k c -> p (a k c)"), WT[b].rearrange("p k c -> p (k c)"), slot.res.ev, writes=[slot])
        return f

    def comp_tm(b):
        def f(slot):
            w = slot.ap.rearrange("p a k c -> p (a k c)").rearrange("p (k c) -> p k c", c=512)
            vs = st.vstage[st.vi % 2]
            st.vi += 1
            for sub in range(4):
                bk = c.bank[st.mbanks[st.oi % len(st.mbanks)]]
                st.oi += 1
                _mmg(k, [(bk[:, 0:512], xnT[:, kc, sub * 128:(sub + 1) * 128], w[:, kc, :], kc == 0, kc == KC - 1) for kc in range(KC)],
                     [xnT, slot], [bk])
                o = vs[:, sub, :]
                if l == 1 and b >= 4:
                    k.op("act", (lambda o=o, bk=bk: lambda h: h.activation(o, bk[:, 0:512], AF.Silu))(), [bk], [vs])
                else:
                    st.ei += 1
                    _copy(k, "act" if st.ei % 2 else "dve", o, bk[:, 0:512], [bk], [vs])
            if l == 1 and b >= 4:
                dst = c.SG[t * TT:(t + 1) * TT, (b - 4) * 512:(b - 3) * 512]
            else:
                dst = c.VS[t * TT:(t + 1) * TT, b * 512:(b + 1) * 512]
            k.dma("pool", dst.rearrange("(s p) c -> p s c", p=128), vs.ap, vs.res.ev, reads=[vs])
        return f

    for b in range(ntm):
        stream.add(comp_tm(b), ld_tm(b), "w")


def phase_p1_l0(c, si, S):
    k, ar = c.k, c.ar
    ar.reset()
    st = Ctx()
    alloc_common(c, st)
    st.tbanks = [0, 1]
    st.mbanks = [2, 3]
    st.cbanks = [4, 5, 6, 7]
    wpool = [ar.alloc(f"w{i}", [4, KC, 128], BF16, ev=k.new_ev("w")) for i in range(3)]
    xpool = [ar.alloc(f"x{i}", [4, D], F32, ev=k.new_ev("x")) for i in range(2)]
    xnT = [ar.alloc(f"xnT{i}", [KC, TT], BF16) for i in range(2)]
    stream = Stream()
    for t in range(S // TT):
        def ldx(slot, t=t):
            k.dma("sp", slot.ap, c.x_in[si][t * TT:(t + 1) * TT, :].rearrange("(s p) d -> p s d", p=128), slot.res.ev, writes=[slot])

        def cx(slot, t=t):
            norm_transpose(c, slot, 0, xnT[t % 2], st)
        stream.add(cx, ldx, "x")
        in_proj_units(c, stream, 0, t, S, xnT[t % 2], st)
    stream.run({"w": wpool, "x": xpool}, hook=lambda: bg_issue(c, 3))
    k.barrier()


def phase_p2_l0(c, si, S):
    k, ar = c.k, c.ar
    ar.reset()
    sc = float(HD ** -0.5)
    slots = []
    for i in range(2):
        sl = Ctx()
        sl.qn = ar.alloc(f"qn{i}", [S], BF16, ev=k.new_ev("qn"))
        sl.kn = ar.alloc(f"kn{i}", [S], BF16, ev=k.new_ev("kn"))
        sl.vp = ar.alloc(f"vp{i}", [S + 2048], BF16, ev=k.new_ev("vp"))
        sl.et = ar.alloc(f"et{i}", [8, 128], F32, ev=k.new_ev("et"))
        slots.append(sl)
    kds = [ar.alloc(f"kd{i}", [S + 2048], BF16) for i in range(2)]
    qds = [ar.alloc(f"qd{i}", [S], BF16) for i in range(2)]
    acc = ar.alloc("acc", [2, S], F32)
    pp = [ar.alloc(f"p{i}", [4, 128], F32) for i in range(4)]
    pm = [ar.alloc(f"pm{i}", [4, 128], BF16) for i in range(4)]
    ucnt = [0]
    mixo = ar.alloc("mixo", [S], BF16, ev=k.new_ev("mixo"))
    stream = Stream()
    cnt = [0]

    def mk_load(h_, g):
        win, dil = DIL[g]
        L = S // dil
        nt = L // 128
        blk = g * 8 + h_

        def f(sl):
            k.dma("sp", sl.qn.ap, c.QKT[blk][:, 0:S], sl.qn.res.ev, writes=[sl.qn])
            k.dma("sp", sl.kn.ap, c.QKT[24 + blk][:, 0:S], sl.kn.res.ev, writes=[sl.kn])
            k.dma("sp", sl.et.ap, c.etab[blk], sl.et.res.ev, writes=[sl.et])
            ev = sl.vp.res.ev
            vp4 = sl.vp[:, 0:dil * (nt + 1) * 128].rearrange("p (r j d) -> p r j d", r=dil, j=nt + 1)
            k.op("pool", lambda h: h.memset(vp4[0:64, :, 0, :], 0.0), [], [sl.vp])
            k.op("pool", lambda h: h.memset(vp4[64:128, :, nt, :], 0.0), [], [sl.vp])
            vcol = c.VS[:, blk * 128:(blk + 1) * 128]
            if nt >= 2:
                if dil == 1:
                    src = vcol[64: 64 + (nt - 1) * 128].rearrange("(j p) d -> p j d", p=128)
                    k.dma("sp", vp4[:, 0, 1:nt, :], src, ev, writes=[sl.vp])
                else:
                    for r in range(dil):
                        src = vcol[64 * dil: 64 * dil + (nt - 1) * 128 * dil].rearrange("(j p r) d -> p r j d", p=128, r=dil)[:, r]
                        k.dma("sp", vp4[:, r, 1:nt, :], src, ev, writes=[sl.vp])
            src = vcol[0:64 * dil].rearrange("(p r) d -> p r d", r=dil)
            k.dma("sp", vp4[64:128, :, 0, :], src, ev, writes=[sl.vp])
            src = vcol[(L - 64) * dil: L * dil].rearrange("(p r) d -> p r d", r=dil)
            k.dma("sp", vp4[0:64, :, nt, :], src, ev, writes=[sl.vp])
        return f

    def mk_comp(h_, g):
        win, dil = DIL[g]
        L = S // dil
        nt = L // 128

        def f(sl):
            kd = kds[ucnt[0] % 2]
            qd = qds[ucnt[0] % 2]
            ucnt[0] += 1
            kd3 = kd[:, 0:dil * (L + 128)].rearrange("p (r i) -> p r i", r=dil)
            vp4 = sl.vp[:, 0:dil * (nt + 1) * 128].rearrange("p (r j d) -> p r j d", r=dil, j=nt + 1)
            k.op("pool", lambda h: h.memset(kd3[:, :, 0:64], 0.0), [], [kd])
            k.op("pool", lambda h: h.memset(kd3[:, :, L + 64:L + 128], 0.0), [], [kd])
            k.op("pool", lambda h: h.tensor_copy(kd3[:, :, 64:64 + L], sl.kn.ap.rearrange("p (i r) -> p r i", r=dil)), [sl.kn], [kd])
            if dil > 1:
                q3 = qd.ap.rearrange("p (r i) -> p r i", r=dil)
                k.op("pool", lambda h: h.tensor_copy(q3, sl.qn.ap.rearrange("p (i r) -> p r i", r=dil)), [sl.qn], [qd])
                qres = qd
            else:
                q3 = sl.qn.ap.rearrange("p (r i) -> p r i", r=1)
                qres = sl.qn
            acc4 = acc.ap.rearrange("p w (i r) -> p w r i", r=dil)
            tiles = [(r, m) for r in range(dil) for m in range(nt)]
            npairs = len(tiles) // 2
            state = {}

            def stage_a(pi):
                n = cnt[0]
                cnt[0] += 1
                bs = c.bank[n % 4]
                bn = c.bank[4 + n % 4]
                p_ = pp[n % 4]
                pm_ = pm[n % 4]
                pr = [tiles[2 * pi], tiles[2 * pi + 1]]
                items = []
                for ti, (r, m) in enumerate(pr):
                    for ab in range(2):
                        items.append((bs[:, (ti * 2 + ab) * 128:(ti * 2 + ab + 1) * 128],
                                      kd3[:, r, 128 * m + 128 * ab: 128 * m + 128 * ab + 128],
                                      q3[:, r, 128 * m:128 * m + 128], True, True))
                _mmg(k, items, [kd, qres], [bs])
                k.op("act", (lambda p_=p_, bs=bs: lambda h: h.activation(p_.ap.rearrange("p a b -> p (a b)"), bs[:, 0:512], AF.Exp, scale=sc))(), [bs], [p_])
                vis = [6 if nt == 1 else (0 if m == 0 else (4 if m == nt - 1 else 2)) for (r, m) in pr]
                if vis[0] == vis[1]:
                    vi = vis[0]
                    ebc = sl.et[:, vi:vi + 2, :].unsqueeze(1).to_broadcast([128, 2, 2, 128])
                    k.op("dve", (lambda p_=p_, pm_=pm_, ebc=ebc: lambda h: h.tensor_tensor(pm_.ap.rearrange("p (t a) b -> p t a b", a=2), p_.ap.rearrange("p (t a) b -> p t a b", a=2), ebc, ALU.mult))(),
                         [p_, sl.et], [pm_])
                else:
                    for ti, vi in enumerate(vis):
                        k.op("dve", (lambda p_=p_, pm_=pm_, ti=ti, vi=vi: lambda h: h.tensor_tensor(pm_[:, 2 * ti:2 * ti + 2, :], p_[:, 2 * ti:2 * ti + 2, :], sl.et[:, vi:vi + 2, :], ALU.mult))(),
                             [p_, sl.et], [pm_])

                state[pi] = (bn, pm_, pr)

            def stage_b(pi):
                bn, pm_, pr = state.pop(pi)
                items = []
                for ti, (r, m) in enumerate(pr):
                    for ab in range(2):
                        items.append((bn[:, ti * 128:(ti + 1) * 128], vp4[:, r, m + ab, :], pm_[:, 2 * ti + ab, :], ab == 0, ab == 1))
                    for ab in range(2):
                        items.append((bn[:, 256 + ti * 128:256 + (ti + 1) * 128], c.ones.ap, pm_[:, 2 * ti + ab, :], ab == 0, ab == 1))
                _mmg(k, items, [sl.vp, pm_, c.ones], [bn])
                (r0, m0) = pr[0]
                if nt >= 2:
                    dst = acc4[:, :, r0, 128 * m0:128 * m0 + 256]
                    src = bn[:, 0:512].rearrange("p (w b) -> p w b", w=2)
                else:
                    dst = acc4[:, :, r0:r0 + 2, 0:128]
                    src = bn[:, 0:512].rearrange("p (w a b) -> p w a b", w=2, a=2)
                if g == 0:
                    k.op("act", (lambda dst=dst, src=src: lambda h: h.activation(dst, src, AF.Copy))(), [bn], [acc])
                else:
                    k.op("dve", (lambda dst=dst, src=src: lambda h: h.tensor_tensor(dst, src, dst, ALU.add))(), [bn, acc], [acc])

            SK = 2
            for idx in range(npairs + SK):
                if idx < npairs:
                    stage_a(idx)
                if idx >= SK:
                    stage_b(idx - SK)
            if g == 2:
                k.op("dve", lambda h: h.reciprocal(acc[:, 1, :], acc[:, 1, :]), [acc], [acc])
                k.op("pool", lambda h: h.tensor_tensor(mixo.ap, acc[:, 0, :], acc[:, 1, :], ALU.mult), [acc], [mixo])
                k.dma("pool", c.MIXT[h_][:, 0:S], mixo.ap, mixo.res.ev, reads=[mixo])
        return f

    for h_ in range(8):
        for g in range(3):
            stream.add(mk_comp(h_, g), mk_load(h_, g), "u")
    stream.run({"u": slots}, hook=lambda: bg_issue(c, 8))
    bg_issue(c, 100000)
    k.bg.clear()
    k.barrier()


def phase_fused(c, si, S, l):
    k, ar = c.k, c.ar
    ar.reset()
    st = Ctx()
    alloc_common(c, st)
    st.tbanks = [0, 1]
    st.mbanks = [2, 3, 4, 5]
    st.cbanks = [6, 7, 0, 1]
    nch = 12 if l == 0 else 20
    WO = c.AWO if l == 0 else c.BWO
    nparts = 1 if l == 0 else 2
    nchp = nch // nparts
    wpool = [ar.alloc(f"w{i}", [4, KC, 128], BF16, ev=k.new_ev("w")) for i in range(3)]
    xpool = [ar.alloc("x1", [4, D], F32, ev=k.new_ev("x1"))]
    mpool = [ar.alloc("mixT", [nch, TT], BF16, ev=k.new_ev("mixT"))]
    xnT = ar.alloc("xnT", [KC, TT], BF16)
    actT = ar.alloc("actT", [22, TT], BF16)
    sgt = [ar.alloc(f"sg{i}", [TT], F32) for i in range(2)]
    if l == 1:
        gbc = ar.alloc("gbc", [D], F32, ev=k.new_ev("gbc"))
        k.dma("sp", gbc.ap, c.gfinal.partition_broadcast(128), gbc.res.ev, writes=[gbc])
    xsrc = c.x_in[si] if l == 0 else c.X1
    stream = Stream()
    gi = [0]

    def flat(slot):
        return slot.ap.rearrange("p a k c -> p (a k c)")

    for t in range(S // TT):
        rows = slice(t * TT, (t + 1) * TT)
        cols = slice(t * TT, (t + 1) * TT)

        def ldx(slot, rows=rows):
            k.dma("sp", slot.ap, xsrc[rows, :].rearrange("(s p) d -> p s d", p=128), slot.res.ev, writes=[slot])

        def cx(slot):
            st.x1 = slot

        def ldm(slot, cols=cols):
            k.dma("sp", slot.ap, c.MIXT[0:nch, :, cols].rearrange("c p s -> p c s"), slot.res.ev, writes=[slot])

        def cm(slot):
            st.mixT = slot
        stream.add(cx, ldx, "x", release=[])
        stream.add(cm, ldm, "m", release=[])

        for cg in range(4):
            for part in range(nparts):
                def ldw(slot, cg=cg, part=part):
                    w = flat(slot)[:, 0:nchp * 512].rearrange("p (j c) -> p j c", c=512)
                    k.dma("sp", w, WO[part * nchp * 128:(part + 1) * nchp * 128, cg * 512:(cg + 1) * 512].rearrange("(j p) c -> p j c", p=128),
                          slot.res.ev, writes=[slot])

                def cw(slot, cg=cg, part=part):
                    w = flat(slot)[:, 0:nchp * 512].rearrange("p (j c) -> p j c", c=512)
                    x1, mixT = st.x1, st.mixT
                    for sub in range(4):
                        bk = c.bank[2 + sub]
                        _mmg(k, [(bk[:, 0:512], mixT[:, part * nchp + j, sub * 128:(sub + 1) * 128], w[:, j, :],
                                  part == 0 and j == 0, part == nparts - 1 and j == nchp - 1) for j in range(nchp)], [mixT, slot], [bk])
                        if part == nparts - 1:
                            o = x1[:, sub, cg * 512:(cg + 1) * 512]
                            k.op("dve", (lambda o=o, bk=bk: lambda h: h.tensor_tensor(o, bk[:, 0:512], o, ALU.add))(), [bk, x1], [x1])
                stream.add(cw, ldw, "w", release=["w"] + (["m"] if (cg == 3 and part == nparts - 1) else []))

        def cnorm(slot):
            norm_transpose(c, st.x1, 4 + l, xnT, st)
        stream.add(cnorm)
        for half in range(2):
            for jj in range(0, 22, 2):
                j0 = half * 22 + jj

                def ldg(slot, j0=j0):
                    k.dma("sp", slot[:, 0:2], c.GU[l][j0:j0 + 2].rearrange("b p k c -> p b k c"), slot.res.ev, writes=[slot])
                    k.dma("sp", slot[:, 2:4], c.GU[l][44 + j0:44 + j0 + 2].rearrange("b p k c -> p b k c"), slot.res.ev, writes=[slot])

                def cg_(slot, jj=jj):
                    for ci in range(2):
                        n = gi[0]
                        gi[0] += 1
                        bg = c.bank[[6, 0][n % 2]]
                        bu = c.bank[[7, 1][n % 2]]
                        sg = sgt[n % 2]
                        _mmg(k, [(bg[:, 0:TT], slot[:, ci, kc, :], xnT[:, kc, :], kc == 0, kc == KC - 1) for kc in range(KC)], [slot, xnT], [bg])
                        _mmg(k, [(bu[:, 0:TT], slot[:, 2 + ci, kc, :], xnT[:, kc, :], kc == 0, kc == KC - 1) for kc in range(KC)], [slot, xnT], [bu])
                        k.op("act", (lambda sg=sg, bg=bg: lambda h: h.activation(sg.ap, bg[:, 0:TT], AF.Silu))(), [bg], [sg])
                        o = actT[:, jj + ci, :]
                        k.op("dve", (lambda o=o, bu=bu, sg=sg: lambda h: h.tensor_tensor(o, bu[:, 0:TT], sg.ap, ALU.mult))(), [bu, sg], [actT])
                stream.add(cg_, ldg, "w")
            for cg in range(4):
                for part in range(2):
                    def ldd(slot, cg=cg, part=part, half=half):
                        w = flat(slot)[:, 0:11 * 512].rearrange("p (j c) -> p j c", c=512)
                        r0 = (half * 22 + part * 11) * 128
                        k.dma("sp", w, c.WD[l][r0:r0 + 11 * 128, cg * 512:(cg + 1) * 512].rearrange("(j p) c -> p j c", p=128), slot.res.ev, writes=[slot])

                    def cd(slot, cg=cg, part=part):
                        w = flat(slot)[:, 0:11 * 512].rearrange("p (j c) -> p j c", c=512)
                        x1 = st.x1
                        for sub in range(4):
                            bk = c.bank[2 + sub]
                            _mmg(k, [(bk[:, 0:512], actT[:, part * 11 + j, sub * 128:(sub + 1) * 128], w[:, j, :],
                                      part == 0 and j == 0, part == 1 and j == 10) for j in range(11)], [actT, slot], [bk])
                            if part == 1:
                                o = x1[:, sub, cg * 512:(cg + 1) * 512]
                                k.op("dve", (lambda o=o, bk=bk: lambda h: h.tensor_tensor(o, bk[:, 0:512], o, ALU.add))(), [bk, x1], [x1])
                    stream.add(cd, ldd, "w")

        if l == 0:
            def cstore(slot, rows=rows):
                x1 = st.x1
                k.dma("pool", c.X1[rows, :].rearrange("(s p) d -> p s d", p=128), x1.ap, x1.res.ev, reads=[x1])
                norm_transpose(c, x1, 1, xnT, st)
            stream.add(cstore, release=["x"])
            in_proj_units(c, stream, 1, t, S, xnT, st)
        else:
            def cfinal(slot, rows=rows):
                x1 = st.x1
                for sub in range(4):
                    ssc = st.ss[:, sub:sub + 1]
                    rsc = st.rstd[:, sub:sub + 1]
                    k.op("dve", (lambda ssc=ssc: lambda h: h.memset(ssc, 0.0))(), [], [st.ss])
                    k.op("act", (lambda ssc=ssc, sub=sub: lambda h: h.activation(st.junk.ap, x1[:, sub, :], AF.Square, accum_out=ssc))(), [x1, st.ss], [st.junk, st.ss])
                    k.op("act", (lambda ssc=ssc, rsc=rsc: lambda h: h.activation(rsc, ssc, AF.Ln, scale=1.0 / D, bias=EPS))(), [st.ss], [st.rstd])
                    k.op("act", (lambda rsc=rsc: lambda h: h.activation(rsc, rsc, AF.Exp, scale=-0.5))(), [st.rstd], [st.rstd])
                    k.op("dve", (lambda rsc=rsc, sub=sub: lambda h: h.scalar_tensor_tensor(x1[:, sub, :], x1[:, sub, :], rsc, gbc.ap, ALU.mult, ALU.mult))(),
                         [x1, st.rstd, gbc], [x1])
                k.dma("pool", c.y_out[si][rows, :].rearrange("(s p) d -> p s d", p=128), x1.ap, x1.res.ev, reads=[x1])
            stream.add(cfinal, release=["x"])
    stream.run({"w": wpool, "x": xpool, "m": mpool})
    k.barrier()


def phase_p2_l1(c, si, S):
    k, ar = c.k, c.ar
    ar.reset()
    n = S // 128
    rscale = float(256 ** -0.5)
    qT = ar.alloc("qT", [2, S], BF16, ev=k.new_ev("qT"))
    kT = ar.alloc("kT", [2, S], BF16, ev=k.new_ev("kT"))
    vt = ar.alloc("vt", [n, 256], BF16, ev=k.new_ev("vt"))
    sg = ar.alloc("sgt", [n, 256], BF16, ev=k.new_ev("sg"))
    oacc = ar.alloc("oacc", [n, 256], F32)
    yT = ar.alloc("yT", [2, S], BF16, ev=k.new_ev("yT"))
    DT = ar.alloc("DT", [128], F32)
    t1 = ar.alloc("t1", [128], F32)
    t2 = ar.alloc("t2", [128], F32)
    Sf = [ar.alloc(f"S{d}", [2, 256], F32) for d in range(2)]
    Sb16 = [[ar.alloc(f"Sb{d}{i}", [2, 256], BF16) for i in range(2)] for d in range(2)]
    pD = [ar.alloc(f"pD{i}", [128], BF16) for i in range(2)]
    kdk = [[ar.alloc(f"kdk{d}{i}", [256], BF16) for i in range(2)] for d in range(2)]
    ss = ar.alloc("ss", [n], F32)
    rstd = ar.alloc("rstd", [n], F32)
    junk = ar.alloc("junk", [256], BF16)
    yn = [ar.alloc(f"yn{i}", [256], BF16) for i in range(2)]
    for h_ in range(8):
        k.dma("sp", qT.ap, c.QKT[2 * h_:2 * h_ + 2, :, 0:S].rearrange("a p s -> p a s"), qT.res.ev, writes=[qT])
        k.dma("sp", kT.ap, c.QKT[16 + 2 * h_:16 + 2 * h_ + 2, :, 0:S].rearrange("a p s -> p a s"), kT.res.ev, writes=[kT])
        k.dma("sp", vt.ap, c.VS[0:S, h_ * 256:(h_ + 1) * 256].rearrange("(n p) c -> p n c", p=128), vt.res.ev, writes=[vt])
        k.dma("sp", sg.ap, c.SG[0:S, h_ * 256:(h_ + 1) * 256].rearrange("(n p) c -> p n c", p=128), sg.res.ev, writes=[sg])
        lgf = c.lg[:, h_:h_ + 1]
        lgb = c.lg[:, 8 + h_:9 + h_]
        k.op("act", (lambda lgf=lgf: lambda h: h.activation(t1.ap, c.rt[:, 0, :], AF.Exp, scale=lgf))(), [c.rt, c.lg], [t1])
        k.op("act", (lambda lgb=lgb: lambda h: h.activation(t2.ap, c.rt[:, 1, :], AF.Exp, scale=lgb))(), [c.rt, c.lg], [t2])
        k.op("dve", lambda h: h.tensor_tensor(t1.ap, t1.ap, c.rt[:, 2, :], ALU.mult), [t1, c.rt], [t1])
        k.op("dve", lambda h: h.tensor_tensor(t2.ap, t2.ap, c.rt[:, 3, :], ALU.mult), [t2, c.rt], [t2])
        k.op("dve", lambda h: h.tensor_tensor(DT.ap, t1.ap, t2.ap, ALU.add), [t1, t2], [DT])
        k.op("dve", lambda h: h.tensor_scalar(DT.ap, DT.ap, rscale, 0.0, ALU.mult, ALU.add), [DT], [DT])
        k.op("pool", lambda h: h.memset(oacc.ap, 0.0), [], [oacc])
        for d in range(2):
            k.op("pool", (lambda d=d: lambda h: h.memset(Sf[d].ap, 0.0))(), [], [Sf[d]])
            k.op("pool", (lambda d=d: lambda h: h.memset(Sb16[d][1].ap, 0.0))(), [], [Sb16[d][1]])

        def chunk_of(d, i):
            return i if d == 0 else n - 1 - i

        def emit_ktrans(d, i):
            ch = chunk_of(d, i)
            cs = slice(ch * 128, (ch + 1) * 128)
            b_t = c.bank[4 + d]
            kk = kdk[d][i % 2]
            kcol = c.kd[:, d, h_:h_ + 1]
            _mmg(k, [(b_t[:, a * 128:(a + 1) * 128], kT[:, a, cs], c.identb.ap, True, True) for a in range(2)], [kT, c.identb], [b_t])
            k.op("act", (lambda kk=kk, b_t=b_t, kcol=kcol: lambda h: h.activation(kk.ap, b_t[:, 0:256], AF.Copy, scale=kcol))(), [b_t, c.kd], [kk])

        for d in range(2):
            emit_ktrans(d, 0)
        for i in range(n):
            for d in range(2):
                ch = chunk_of(d, i)
                cs = slice(ch * 128, (ch + 1) * 128)
                qcol = c.kd[:, 2 + d, h_:h_ + 1]
                gcol_ = c.gC[:, d * 8 + h_: d * 8 + h_ + 1]
                b_o = c.bank[2 + d]
                b_s = c.bank[6 + d]
                kk = kdk[d][i % 2]
                s_prev = Sb16[d][(i + 1) % 2]
                s_next = Sb16[d][i % 2]
                o = oacc[:, ch, :]
                if i + 1 < n:
                    emit_ktrans(d, i + 1)
                if d == 0:
                    b_sc = c.bank[i % 2]
                    pd_ = pD[i % 2]
                    _mmg(k, [(b_sc[:, 0:128], kT[:, a, cs], qT[:, a, cs], a == 0, a == 1) for a in range(2)], [kT, qT], [b_sc])
                    k.op("dve", (lambda pd_=pd_, b_sc=b_sc: lambda h: h.tensor_tensor(pd_.ap, b_sc[:, 0:128], DT.ap, ALU.mult))(), [b_sc, DT], [pd_])
                _mmg(k, [(b_s[:, a * 256:(a + 1) * 256], kk[:, a * 128:(a + 1) * 128], vt[:, ch, :], True, True) for a in range(2)], [kk, vt], [b_s])
                _mmg(k, [(b_o[:, 0:256], qT[:, a, cs], s_prev[:, a, :], a == 0, a == 1) for a in range(2)], [qT, s_prev], [b_o])
                if d == 0:
                    _mmg(k, [(b_o[:, 256:512], pd_.ap, vt[:, ch, :], True, True)], [pd_, vt], [b_o])
                    k.op("dve", (lambda o=o, b_o=b_o: lambda h: h.tensor_tensor(o, b_o[:, 256:512], o, ALU.add))(), [b_o, oacc], [oacc])
                k.op("dve", (lambda o=o, b_o=b_o, qcol=qcol: lambda h: h.scalar_tensor_tensor(o, b_o[:, 0:256], qcol, o, ALU.mult, ALU.add))(), [b_o, oacc, c.kd], [oacc])
                sfl = Sf[d].ap.rearrange("p a b -> p (a b)")
                k.op("dve", (lambda sfl=sfl, b_s=b_s, gcol_=gcol_: lambda h: h.scalar_tensor_tensor(sfl, sfl, gcol_, b_s[:, 0:512], ALU.mult, ALU.add))(), [b_s, Sf[d], c.gC], [Sf[d]])
                k.op("pool", (lambda d=d, s_next=s_next: lambda h: h.tensor_copy(s_next.ap, Sf[d].ap))(), [Sf[d]], [s_next])
        k.op("dve", lambda h: h.memset(ss.ap, 0.0), [], [ss])
        for ch in range(n):
            ssc = ss[:, ch:ch + 1]
            o = oacc[:, ch, :]
            k.op("act", (lambda ssc=ssc, o=o: lambda h: h.activation(junk.ap, o, AF.Square, accum_out=ssc))(), [oacc, ss], [junk, ss])
        k.op("act", lambda h: h.activation(rstd.ap, ss.ap, AF.Ln, scale=1.0 / 256, bias=EPS), [ss], [rstd])
        k.op("act", lambda h: h.activation(rstd.ap, rstd.ap, AF.Exp, scale=-0.5), [rstd], [rstd])
        for ch in range(n):
            cs = slice(ch * 128, (ch + 1) * 128)
            rsc = rstd[:, ch:ch + 1]
            o = oacc[:, ch, :]
            yn_ = yn[ch % 2]
            b_y = c.bank[ch % 2]
            k.op("dve", (lambda yn_=yn_, o=o, rsc=rsc, ch=ch: lambda h: h.scalar_tensor_tensor(yn_.ap, o, rsc, sg[:, ch, :], ALU.mult, ALU.mult))(), [oacc, rstd, sg], [yn_])
            _mmg(k, [(b_y[:, a * 128:(a + 1) * 128], yn_[:, a * 128:(a + 1) * 128], c.identb.ap, True, True) for a in range(2)], [yn_, c.identb], [b_y])
            k.op("act", (lambda b_y=b_y, cs=cs: lambda h: h.activation(yT[:, :, cs], b_y[:, 0:256].rearrange("p (a b) -> p a b", b=128), AF.Copy))(), [b_y], [yT])
        k.dma("pool", c.MIXT[2 * h_:2 * h_ + 2, :, 0:S].rearrange("a p s -> p a s"), yT.ap, yT.res.ev, reads=[yT])
    k.barrier()


def run_sequence(c, si, S):
    import os
    stop = int(os.environ.get("KDBG_STOP", "99"))
    memkv(c, si)
    if stop < 1:
        return
    phase_p1_l0(c, si, S)
    if stop < 2:
        return
    phase_p2_l0(c, si, S)
    if stop < 3:
        return
    phase_fused(c, si, S, 0)
    if stop < 4:
        return
    phase_p2_l1(c, si, S)
    if stop < 5:
        return
    phase_fused(c, si, S, 1)


def _const_tables():
    a = np.arange(128, dtype=np.float64)[:, None]
    b = np.arange(128, dtype=np.float64)[None, :]
    etab = np.zeros((24, 128, 8, 128), np.float32)
    for g, (win, dil) in enumerate(DIL):
        for h_ in range(8):
            slope = 2.0 ** (-8.0 * (g * 8 + h_ + 1) / 24.0)
            A = np.where(a >= b, np.exp(-slope * dil * np.abs(a - 64 - b)), 0.0)
            B = np.where(a <= b, np.exp(-slope * dil * np.abs(a + 64 - b)), 0.0)
            Af = A * (a >= 64)
            Bl = B * (a < 64)
            for i, t in enumerate((Af, B, A, B, A, Bl, Af, Bl)):
                etab[g * 8 + h_, :, i, :] = t
    rtab = np.zeros((128, 4, 128), np.float32)
    rtab[:, 0, :] = np.maximum(b - a, 0)
    rtab[:, 1, :] = np.maximum(a - b, 0)
    rtab[:, 2, :] = (b >= a)
    rtab[:, 3, :] = (a > b)
    p = np.arange(128, dtype=np.float32)
    ccol = np.stack([127 - p, p, p + 1, 128 - p], axis=1).astype(np.float32)
    return etab, rtab, ccol


_NC_CACHE = {}


def kernel(x_prompt, x_sample, mem_prompt, mem_sample, norm_mix, norm_mem, w_mem_kv, a_w_in, a_w_out,
           b_w_in, b_w_out, b_decay_fwd, b_decay_bwd, norm_ffn, w_gate_up, w_down, norm_final):
    f = lambda a: np.ascontiguousarray(np.asarray(a, dtype=np.float32))
    S_list = (x_prompt.shape[1], x_sample.shape[1])
    if S_list not in _NC_CACHE:
        _NC_CACHE[S_list] = build(list(S_list))
    nc = _NC_CACHE[S_list]
    etab, rtab, ccol = _const_tables()

    def col(g):
        return f(g).reshape(16, 128).T

    gcols = np.ascontiguousarray(np.stack([col(norm_mix[0]), col(norm_mix[1]), col(norm_mem[0]), col(norm_mem[1]),
                                           col(norm_ffn[0]), col(norm_ffn[1]), col(norm_final)], axis=0))
    shared = {
        "gcols": gcols, "gfinal": f(norm_final).reshape(1, D),
        "decay": np.ascontiguousarray(np.concatenate([f(b_decay_fwd).reshape(-1), f(b_decay_bwd).reshape(-1)]).reshape(1, 16)),
        "ident": np.eye(128, dtype=np.float32), "etab": etab, "rtab": rtab, "ccol": ccol,
        "w_mem_kv": f(w_mem_kv), "a_w_in": f(a_w_in)[0], "a_w_out": f(a_w_out)[0], "b_w_in": f(b_w_in)[0],
        "b_w_out": f(b_w_out)[0], "w_gate_up": f(w_gate_up), "w_down": f(w_down),
    }
    n = x_prompt.shape[0]
    in_maps = []
    for i in range(n):
        m = dict(shared)
        m["x0"] = f(x_prompt[i])
        m["x1"] = f(x_sample[i])
        m["mem0"] = f(mem_prompt[i])
        m["mem1"] = f(mem_sample[i])
        in_maps.append(m)
    res = run_bass_kernel_spmd(nc, in_maps, core_ids=list(range(n)))
    y_p = np.stack([res.results[i]["y0"] for i in range(n)], axis=0).astype(np.float32)
    y_s = np.stack([res.results[i]["y1"] for i in range(n)], axis=0).astype(np.float32)
    return (y_p, y_s)
```

```python
import numpy as np
import concourse.bass as bass
import concourse.mybir as mybir
from concourse.bass_utils import run_bass_kernel_spmd

F32 = mybir.dt.float32
BF16 = mybir.dt.bfloat16
U8 = mybir.dt.uint8
AF = mybir.ActivationFunctionType
ALU = mybir.AluOpType
AX = mybir.AxisListType

D = 2048
KC = 16
HD = 128
DFF = 5632
NFF = 44
A_IN = 9728
B_IN = 8704
EPS = 1e-6
DIL = ((128, 1), (512, 4), (2048, 16))
TT = 512
ARENA = 200 * 1024
PARANOID = True


class Ev:
    def __init__(self, sem, name):
        self.sem = sem
        self.count = 0
        self.name = name


class Res:
    __slots__ = ("name", "w", "r", "ev", "excl")

    def __init__(self, name, excl=False):
        self.name = name
        self.w = {}
        self.r = {}
        self.ev = None
        self.excl = excl


class Tile:
    def __init__(self, ap, res):
        self.ap = ap
        self.res = res

    def __getitem__(self, k):
        return self.ap[k]


class Eng:
    def __init__(self, name, ev):
        self.name = name
        self.ev = ev
        self.seen = {}
        self.q = []


class K:
    def __init__(self, nc, sems):
        self.nc = nc
        self.sems = list(sems)
        self.eng = {}
        for n in ("pe", "act", "dve", "pool", "sp"):
            self.eng[n] = Eng(n, Ev(self.sems.pop(), n))
        self.dma_evs = []
        self.free_evs = []
        self.used_evs = []
        self.bg = set()
        self.nops = 0

    def new_ev(self, name):
        if self.free_evs:
            ev = self.free_evs.pop()
        else:
            ev = Ev(self.sems.pop(), name)
            self.dma_evs.append(ev)
        self.used_evs.append(ev)
        return ev

    def _waits(self, e, reads, writes, skip=None):
        deps = {}
        for t in reads:
            r = t.res if isinstance(t, Tile) else t
            for ev, v in r.w.items():
                if deps.get(ev, 0) < v:
                    deps[ev] = v
        for t in writes:
            r = t.res if isinstance(t, Tile) else t
            for ev, v in r.w.items():
                if ev is skip:
                    continue
                if deps.get(ev, 0) < v:
                    deps[ev] = v
            for ev, v in r.r.items():
                if deps.get(ev, 0) < v:
                    deps[ev] = v
        for ev, v in deps.items():
            if ev is e.ev and (e.name == "pe" or not PARANOID):
                continue
            if e.seen.get(ev, 0) >= v:
                continue
            e.seen[ev] = v
            e.q.append(("wait", ev.sem, v))

    def _mark(self, ev, val, reads, writes):
        for t in reads:
            r = t.res if isinstance(t, Tile) else t
            r.r[ev] = val
        for t in writes:
            r = t.res if isinstance(t, Tile) else t
            r.w = {ev: val}
            r.r = {}

    def op(self, en, fn, reads=(), writes=()):
        e = self.eng[en]
        ex = [t for t in reads if (t.res if isinstance(t, Tile) else t).excl]
        if ex:
            reads = [t for t in reads if not (t.res if isinstance(t, Tile) else t).excl]
            writes = list(writes) + ex
        self._waits(e, reads, writes)
        e.ev.count += 1
        e.q.append(("op", fn, e.ev.sem))
        self._mark(e.ev, e.ev.count, reads, writes)
        self.nops += 1

    def dma(self, qn, out_ap, in_ap, ev, reads=(), writes=()):
        e = self.eng[qn]
        self._waits(e, reads, writes, skip=ev)
        ev.count += 16
        e.q.append(("dma", out_ap, in_ap, ev.sem))
        self._mark(ev, ev.count, reads, writes)

    def barrier(self):
        sp = self.eng["sp"]
        for ev in self.dma_evs:
            if ev in self.bg:
                continue
            if ev.count > sp.seen.get(ev, 0):
                sp.seen[ev] = ev.count
                sp.q.append(("wait", ev.sem, ev.count))
        sp.ev.count += 1
        sp.q.append(("inc", sp.ev.sem))
        for e in self.eng.values():
            for f in self.eng.values():
                if f is e:
                    continue
                if f.ev.count > e.seen.get(f.ev, 0):
                    e.seen[f.ev] = f.ev.count
                    e.q.append(("wait", f.ev.sem, f.ev.count))
        self.free_evs.extend(ev for ev in self.used_evs if ev not in self.bg)
        self.used_evs = [ev for ev in self.used_evs if ev in self.bg]

    def replay(self, name, h):
        for it in self.eng[name].q:
            k = it[0]
            if k == "wait":
                h.wait_ge(it[1], it[2])
            elif k == "op":
                ins = it[1](h)
                ins.then_inc(it[2], 1)
            elif k == "dma":
                h.dma_start(out=it[1], in_=it[2]).then_inc(it[3], 16)
            elif k == "inc":
                h.sem_inc(it[1], 1)


class Arena:
    def __init__(self, base_ap, size):
        self.base = base_ap
        self.size = size
        self.off = 0
        self.mark_ = 0

    def alloc(self, name, free_shape, dtype, ev=None):
        esz = {F32: 4, BF16: 2, U8: 1}[dtype]
        n = 1
        for s in free_shape:
            n *= s
        nb = (n * esz + 63) // 64 * 64
        assert self.off + nb <= self.size, f"arena overflow at {name}: {self.off}+{nb}"
        ap = self.base[:, self.off:self.off + n * esz]
        if dtype != U8:
            ap = ap.bitcast(dtype)
        if len(free_shape) == 2:
            ap = ap.rearrange("p (a b) -> p a b", b=free_shape[1])
        elif len(free_shape) == 3:
            ap = ap.rearrange("p (a b c) -> p a b c", b=free_shape[1], c=free_shape[2])
        elif len(free_shape) == 4:
            ap = ap.rearrange("p (a b c d) -> p a b c d", b=free_shape[1], c=free_shape[2], d=free_shape[3])
        self.off += nb
        r = Res(name)
        r.ev = ev
        return Tile(ap, r)

    def mark(self):
        self.mark_ = self.off

    def reset(self):
        self.off = self.mark_


class Stream:
    def __init__(self):
        self.units = []

    def add(self, compute, load=None, pool=None, release=None):
        if release is None:
            release = [pool] if load is not None else []
        self.units.append((compute, load, pool, release))

    def run(self, pools, lookahead=8, hook=None):
        loads = [(i, u) for i, u in enumerate(self.units) if u[1] is not None]
        issued = {p: 0 for p in pools}
        consumed = {p: 0 for p in pools}
        slot_of = {}
        li = 0
        for i, (compute, load, pool, release) in enumerate(self.units):
            while li < len(loads):
                j, (c2, l2, p2, r2) = loads[li]
                if j > i + lookahead:
                    break
                if issued[p2] - consumed[p2] >= len(pools[p2]):
                    assert j != i, "stream deadlock"
                    break
                slot = pools[p2][issued[p2] % len(pools[p2])]
                issued[p2] += 1
                slot_of[j] = slot
                l2(slot)
                li += 1
            compute(slot_of.get(i))
            if hook is not None:
                hook()
            for p in release:
                consumed[p] += 1


def _mmg(k, items, reads, writes):
    items = list(items)

    def fn(h):
        ins = None
        for (o, l, r, s, e) in items:
            ins = h.matmul(o, l, r, start=s, stop=e)
        return ins
    k.op("pe", fn, reads, writes)


def _copy(k, en, out, in_, reads, writes):
    if en == "act":
        k.op("act", lambda h: h.activation(out, in_, AF.Copy), reads, writes)
    else:
        k.op(en, lambda h: h.tensor_copy(out, in_), reads, writes)


class Ctx:
    pass


def build(S_list):
    from contextlib import ExitStack
    nc = bass.Bass("TRN2", target_bir_lowering=False)
    c = Ctx()
    c.nc = nc

    def din(name, shape):
        return nc.dram_tensor(name, shape, F32, kind="ExternalInput").ap()

    import os
    _dbg = os.environ.get("KDBG_OUT", "").split(",")

    def dscr(name, shape, dt=BF16):
        return nc.dram_tensor(name, shape, dt, kind=("ExternalOutput" if name in _dbg else "Internal")).ap()

    SMAX = max(S_list)
    c.x_in = [din(f"x{i}", [S, D]) for i, S in enumerate(S_list)]
    c.mem_in = [din(f"mem{i}", [256, D]) for i, S in enumerate(S_list)]
    c.y_out = [nc.dram_tensor(f"y{i}", [S, D], F32, kind="ExternalOutput").ap() for i, S in enumerate(S_list)]
    c.gcols = din("gcols", [7, 128, 16])
    c.gfinal = din("gfinal", [1, D])
    c.decay = din("decay", [1, 16])
    c.ident = din("ident", [128, 128])
    c.etab = din("etab", [24, 128, 8, 128])
    c.rtab = din("rtab", [128, 4, 128])
    c.ccol = din("ccol", [128, 4])
    w_mem_kv = din("w_mem_kv", [2, D, 1024])
    a_w_in = din("a_w_in", [D, A_IN])
    a_w_out = din("a_w_out", [1536, D])
    b_w_in = din("b_w_in", [D, B_IN])
    b_w_out = din("b_w_out", [2560, D])
    w_gate_up = din("w_gate_up", [2, D, 2 * DFF])
    w_down = din("w_down", [2, DFF, D])
    c.AWF = dscr("AWF", [52, 128, KC, 128])
    c.AWT = dscr("AWT", [6, 128, KC, 512])
    c.BWF = dscr("BWF", [36, 128, KC, 128])
    c.BWT = dscr("BWT", [8, 128, KC, 512])
    c.GU = [dscr(f"GU{l}", [88, 128, KC, 128]) for l in range(2)]
    c.WD = [dscr(f"WD{l}", [DFF, D]) for l in range(2)]
    c.AWO = dscr("AWO", [1536, D])
    c.BWO = dscr("BWO", [2560, D])
    c.MKF = [dscr(f"MKF{l}", [4, 128, KC, 128]) for l in range(2)]
    c.MKT = [dscr(f"MKT{l}", [128, KC, 512]) for l in range(2)]
    c.QKT = dscr("QKT", [48, 128, SMAX])
    c.VS = dscr("VS", [SMAX, 3072])
    c.SG = dscr("SG", [SMAX, 2048])
    c.MIXT = dscr("MIXT", [20, 128, SMAX])
    c.X1 = dscr("X1", [SMAX, D], F32)

    with ExitStack() as es:
        arena_t = es.enter_context(nc.sbuf_tensor("arena", [128, ARENA], U8))
        ps_t = es.enter_context(nc.psum_tensor("ps", [128, 4096], F32))
        sems = [es.enter_context(nc.semaphore(f"s{i}")) for i in range(48)]
        k = K(nc, sems)
        c.k = k
        ar = Arena(arena_t, ARENA)
        c.ar = ar
        c.bank = [Tile(ps_t[:, b * 512:(b + 1) * 512], Res(f"bank{b}", excl=True)) for b in range(8)]

        pev = k.new_ev("prologue")

        def cast_fm(dst, dst0, src, col0, nblk):
            for b in range(nblk):
                s = src[:, col0 + b * 128: col0 + (b + 1) * 128].rearrange("(kc p) c -> p kc c", p=128)
                k.dma("pool", dst[dst0 + b], s, pev)

        def cast_tm(dst_ap, src, col0):
            s = src[:, col0: col0 + 512].rearrange("(kc p) c -> p kc c", p=128)
            k.dma("pool", dst_ap, s, pev)

        def cast_nat(dst, src, rows):
            step = 512
            for r0 in range(0, rows, step):
                r1 = min(rows, r0 + step)
                k.dma("pool", dst[r0:r1, :], src[r0:r1, :], pev)

        for l in range(2):
            cast_fm(c.MKF[l], 0, w_mem_kv[l], 0, 4)
            cast_tm(c.MKT[l], w_mem_kv[l], 512)
        cast_fm(c.AWF, 0, a_w_in, 0, 24)
        cast_fm(c.AWF, 24, a_w_in, 3072, 24)
        cast_fm(c.AWF, 48, a_w_in, 9216, 4)
        for b in range(6):
            cast_tm(c.AWT[b], a_w_in, 6144 + b * 512)
        def prologue_part2():
          pev = k.new_ev("prologue2")
          k.bg.add(pev)
          c.bg_ev = pev
          c.bg_casts = []

          def cast_fm(dst, dst0, src, col0, nblk):
            for b in range(nblk):
                s_ = src[:, col0 + b * 128: col0 + (b + 1) * 128].rearrange("(kc p) c -> p kc c", p=128)
                c.bg_casts.append((dst[dst0 + b], s_))

          def cast_tm(dst_ap, src, col0):
            s_ = src[:, col0: col0 + 512].rearrange("(kc p) c -> p kc c", p=128)
            c.bg_casts.append((dst_ap, s_))

          def cast_nat(dst, src, rows):
            step = 512
            for r0 in range(0, rows, step):
                r1 = min(rows, r0 + step)
                c.bg_casts.append((dst[r0:r1, :], src[r0:r1, :]))
          cast_nat(c.AWO, a_w_out, 1536)
          for l in range(2):
            cast_fm(c.GU[l], 0, w_gate_up[l], 0, 88)
            cast_nat(c.WD[l], w_down[l], DFF)
          cast_fm(c.BWF, 0, b_w_in, 0, 16)
          cast_fm(c.BWF, 16, b_w_in, 2048, 16)
          cast_fm(c.BWF, 32, b_w_in, 8192, 4)
          for b in range(8):
            cast_tm(c.BWT[b], b_w_in, 4096 + b * 512)
          cast_nat(c.BWO, b_w_out, 2560)

        _stop = int(os.environ.get('KDBG_STOP', '99'))
        lev = k.new_ev("constload")
        identf = ar.alloc("identf", [128], F32)
        c.identb = ar.alloc("identb", [128], BF16)
        c.ones = ar.alloc("ones", [128], BF16)
        c.gcol = ar.alloc("gcol", [7, KC], F32)
        c.rt = ar.alloc("rtab", [4, 128], F32)
        c.cc = ar.alloc("ccol", [4], F32)
        c.dec = ar.alloc("dec", [16], F32)
        c.lg = ar.alloc("lg", [16], F32)
        c.kd = ar.alloc("kdcols", [4, 8], F32)
        c.gC = ar.alloc("gC", [16], F32)
        c.zero = ar.alloc("zero", [1], F32)
        k.dma("sp", identf.ap, c.ident, lev, writes=[identf])
        k.dma("sp", c.gcol.ap, c.gcols.rearrange("g p k -> p g k"), k.new_ev("c1"), writes=[c.gcol])
        k.dma("sp", c.rt.ap, c.rtab, k.new_ev("c2"), writes=[c.rt])
        k.dma("sp", c.cc.ap, c.ccol, k.new_ev("c3"), writes=[c.cc])
        k.dma("sp", c.dec.ap, c.decay.partition_broadcast(128), k.new_ev("c4"), writes=[c.dec])
        if _stop == -3:
            S_list = []
        k.op("dve", lambda h: h.tensor_copy(c.identb.ap, identf.ap), [identf], [c.identb])
        k.op("dve", lambda h: h.memset(c.ones.ap, 1.0), [], [c.ones])
        k.op("dve", lambda h: h.memset(c.zero.ap, 0.0), [], [c.zero])
        if _stop == -2:
            S_list = []
        k.op("act", lambda h: h.activation(c.lg.ap, c.dec.ap, AF.Exp, scale=-float(np.log(2.0))), [c.dec], [c.lg])
        k.op("act", lambda h: h.activation(c.lg.ap, c.lg.ap, AF.Ln, scale=-1.0, bias=1.0), [c.lg], [c.lg])
        rscale = float(256 ** -0.5)
        for h_ in range(8):
            for d_, (tabi, outi, mul) in enumerate([(0, 0, 1.0), (1, 1, 1.0), (2, 2, rscale), (3, 3, rscale)]):
                lgc = c.lg[:, (h_ if d_ in (0, 2) else 8 + h_):(h_ if d_ in (0, 2) else 8 + h_) + 1]
                o = c.kd[:, outi, h_:h_ + 1]
                i_ = c.cc[:, tabi:tabi + 1]
                k.op("act", (lambda o=o, i_=i_, lgc=lgc: (lambda h: h.activation(o, i_, AF.Exp, scale=lgc)))(), [c.cc, c.lg], [c.kd])
        k.op("dve", lambda h: h.tensor_scalar(c.kd[:, 2:4, :], c.kd[:, 2:4, :], rscale, 0.0, ALU.mult, ALU.add), [c.kd], [c.kd])
        k.op("act", lambda h: h.activation(c.gC.ap, c.lg.ap, AF.Exp, scale=128.0), [c.lg], [c.gC])
        c.kmemT = [ar.alloc(f"kmemT{l}", [4, 256], BF16) for l in range(2)]
        c.vmem = [ar.alloc(f"vmem{l}", [2, 512], BF16) for l in range(2)]
        ar.mark()
        k.barrier()
        prologue_part2()

        if _stop == -1:
            S_list = []
        for si, S in enumerate(S_list):
            run_sequence(c, si, S)

        k.barrier()
        c.stats = {n: len(e.q) for n, e in k.eng.items()}
        nc._kstats = c.stats
        with nc.Block() as block:
            @block.tensor
            def _(h):
                k.replay("pe", h)

            @block.scalar
            def _(h):
                k.replay("act", h)

            @block.vector
            def _(h):
                k.replay("dve", h)

            @block.gpsimd
            def _(h):
                k.replay("pool", h)

            @block.sync
            def _(h):
                k.replay("sp", h)
    return nc


def norm_transpose(c, xt, gidx, xnT, st, nsub=4, width=TT):
    k = c.k
    k.op("dve", lambda h: h.memset(st.ss[:, 0:nsub], 0.0), [], [st.ss])
    for sub in range(nsub):
        ssc = st.ss[:, sub:sub + 1]
        k.op("act", (lambda ssc=ssc, sub=sub: lambda h: h.activation(st.junk.ap, xt[:, sub, :], AF.Square, accum_out=ssc))(),
             [xt, st.ss], [st.junk, st.ss])
    k.op("act", lambda h: h.activation(st.rstd[:, 0:nsub], st.ss[:, 0:nsub], AF.Ln, scale=1.0 / D, bias=EPS), [st.ss], [st.rstd])
    k.op("act", lambda h: h.activation(st.rstd[:, 0:nsub], st.rstd[:, 0:nsub], AF.Exp, scale=-0.5), [st.rstd], [st.rstd])
    for sub in range(nsub):
        rsc = st.rstd[:, sub:sub + 1]
        xs = st.xs[sub % 2]
        k.op("dve", (lambda xs=xs, rsc=rsc, sub=sub: lambda h: h.tensor_scalar(xs.ap, xt[:, sub, :], rsc, 0.0, ALU.mult, ALU.add))(),
             [xt, st.rstd], [xs])
        for q4 in range(4):
            bk = c.bank[st.tbanks[(sub * 4 + q4) % len(st.tbanks)]]
            _mmg(k, [(bk[:, j * 128:(j + 1) * 128], xs[:, (q4 * 4 + j) * 128:(q4 * 4 + j + 1) * 128], c.identb.ap, True, True)
                     for j in range(4)], [xs, c.identb], [bk])
            for j in range(4):
                kc = q4 * 4 + j
                o = xnT[:, kc, sub * 128:(sub + 1) * 128]
                i_ = bk[:, j * 128:(j + 1) * 128]
                gs = c.gcol[:, gidx, kc:kc + 1]
                if q4 % 2 == 0:
                    k.op("act", (lambda o=o, i_=i_, gs=gs: lambda h: h.activation(o, i_, AF.Copy, scale=gs))(), [bk, c.gcol], [xnT])
                else:
                    k.op("dve", (lambda o=o, i_=i_, gs=gs: lambda h: h.tensor_scalar(o, i_, gs, 0.0, ALU.mult, ALU.add))(), [bk, c.gcol], [xnT])


def fm_block_mm(c, bk, w_ap, xnT, width=TT):
    _mmg(c.k, [(bk[:, 0:width], w_ap[:, kc, :], xnT[:, kc, 0:width], kc == 0, kc == KC - 1) for kc in range(KC)],
         [xnT, c.cur_wres], [bk])


def cross_attn(c, l, hc, qcT, outT, st, width=TT):
    k = c.k
    sc = float(HD ** -0.5)
    b_s = [c.bank[st.cbanks[0]], c.bank[st.cbanks[1]]]
    b_n = c.bank[st.cbanks[2]]
    b_d = c.bank[st.cbanks[3]]
    for kt in range(2):
        _mmg(k, [(b_s[kt][:, 0:width], c.kmemT[l][:, hc, kt * 128:(kt + 1) * 128], qcT[:, 0:width], True, True)], [c.kmemT[l], qcT], [b_s[kt]])
        o = st.pT[:, kt, 0:width]
        i_ = b_s[kt][:, 0:width]
        k.op("act", (lambda o=o, i_=i_: lambda h: h.activation(o, i_, AF.Exp, scale=sc))(), [b_s[kt]], [st.pT])
    _mmg(k, [(b_n[:, 0:width], c.vmem[l][:, kt, hc * 128:(hc + 1) * 128], st.pT[:, kt, 0:width], kt == 0, kt == 1) for kt in range(2)],
         [c.vmem[l], st.pT], [b_n])
    _mmg(k, [(b_d[:, 0:width], c.ones.ap, st.pT[:, kt, 0:width], kt == 0, kt == 1) for kt in range(2)], [c.ones, st.pT], [b_d])
    k.op("dve", lambda h: h.reciprocal(st.rden[:, 0:width], b_d[:, 0:width]), [b_d], [st.rden])
    k.op("dve", lambda h: h.tensor_tensor(outT[:, 0:width], b_n[:, 0:width], st.rden[:, 0:width], ALU.mult), [b_n, st.rden], [outT])


def bg_issue(c, n):
    lst = getattr(c, "bg_casts", None)
    while lst and n > 0:
        dst, src = lst.pop(0)
        c.k.dma("pool", dst, src, c.bg_ev)
        n -= 1


def memkv(c, si):
    k, ar = c.k, c.ar
    ar.reset()
    ev = k.new_ev("memload")
    mt = ar.alloc("mem", [2, D], F32)
    mT = ar.alloc("mT", [KC, 256], BF16)
    wf = ar.alloc("wf", [4, KC, 128], BF16, ev=k.new_ev("wf"))
    wt = ar.alloc("wt", [KC, 512], BF16, ev=k.new_ev("wt"))
    st = Ctx()
    st.ss = ar.alloc("ss", [4], F32)
    st.rstd = ar.alloc("rstd", [4], F32)
    st.junk = ar.alloc("junk", [D], BF16)
    st.xs = [ar.alloc(f"xs{i}", [D], BF16) for i in range(2)]
    st.tbanks = [0, 1]
    k.dma("sp", mt.ap, c.mem_in[si].rearrange("(s p) d -> p s d", p=128), ev, writes=[mt])
    for l in range(2):
        norm_transpose(c, mt, 2 + l, mT, st, nsub=2, width=256)
        k.dma("sp", wf.ap, c.MKF[l].rearrange("b p k c -> p b k c"), wf.res.ev, writes=[wf])
        k.dma("sp", wt.ap, c.MKT[l], wt.res.ev, writes=[wt])
        for hc in range(4):
            bk = c.bank[2 + hc % 2]
            _mmg(k, [(bk[:, 0:256], wf[:, hc, kc, :], mT[:, kc, :], kc == 0, kc == KC - 1) for kc in range(KC)], [wf, mT], [bk])
            o = c.kmemT[l][:, hc, :]
            _copy(k, "act", o, bk[:, 0:256], [bk], [c.kmemT[l]])
        for kt in range(2):
            bk = c.bank[4 + kt]
            _mmg(k, [(bk[:, 0:512], mT[:, kc, kt * 128:(kt + 1) * 128], wt[:, kc, :], kc == 0, kc == KC - 1) for kc in range(KC)], [wt, mT], [bk])
            o = c.vmem[l][:, kt, :]
            _copy(k, "dve", o, bk[:, 0:512], [bk], [c.vmem[l]])
    k.barrier()


def alloc_common(c, st, nbig=True):
    ar = c.ar
    st.ss = ar.alloc("ss", [4], F32)
    st.rstd = ar.alloc("rstd", [4], F32)
    st.junk = ar.alloc("junk", [D], BF16)
    st.xs = [ar.alloc(f"xs{i}", [D], BF16) for i in range(2)]
    st.pT = ar.alloc("pT", [2, TT], BF16)
    st.rden = ar.alloc("rden", [TT], F32)
    st.qcT = [ar.alloc(f"qcT{i}", [TT], BF16) for i in range(2)]
    st.crossT = [ar.alloc(f"crossT{i}", [TT], BF16, ev=c.k.new_ev("crossT")) for i in range(2)]
    st.ostage = [ar.alloc(f"ost{i}", [TT], BF16, ev=c.k.new_ev("ostage")) for i in range(4)]
    st.vstage = [ar.alloc(f"vst{i}", [4, TT], BF16, ev=c.k.new_ev("vstage")) for i in range(2)]
    st.oi = 0
    st.vi = 0
    st.ci = 0
    st.ei = 0


def in_proj_units(c, stream, l, t, S, xnT, st):
    k = c.k
    WF = c.AWF if l == 0 else c.BWF
    WT = c.AWT if l == 0 else c.BWT
    nqk = 48 if l == 0 else 32
    cross_base = 8 if l == 0 else 16
    cols = slice(t * TT, (t + 1) * TT)

    def ld_fm(b0):
        def f(slot):
            k.dma("sp", slot[:, 0:4], WF[b0:b0 + 4].rearrange("b p k c -> p b k c"), slot.res.ev, writes=[slot])
        return f

    def evac(bk, o):
        st.ei += 1
        _copy(k, "act" if st.ei % 2 else "dve", o, bk[:, 0:TT], [bk], [st.cur_o])

    def comp_qk(b0):
        def f(slot):
            c.cur_wres = slot
            for j in range(4):
                bk = c.bank[st.mbanks[st.oi % len(st.mbanks)]]
                og = st.ostage[st.oi % 4]
                st.oi += 1
                fm_block_mm(c, bk, slot[:, j], xnT)
                st.cur_o = og
                evac(bk, og.ap)
                k.dma("pool", c.QKT[b0 + j][:, cols], og.ap, og.res.ev, reads=[og])
        return f

    def comp_qc(slot):
        c.cur_wres = slot
        for hc in range(4):
            bk = c.bank[st.mbanks[st.oi % len(st.mbanks)]]
            st.oi += 1
            qc = st.qcT[st.ci % 2]
            ct = st.crossT[st.ci % 2]
            st.ci += 1
            fm_block_mm(c, bk, slot[:, hc], xnT)
            st.cur_o = qc
            evac(bk, qc.ap)
            cross_attn(c, l, hc, qc, ct, st)
            k.dma("pool", c.MIXT[cross_base + hc][:, cols], ct.ap, ct.res.ev, reads=[ct])

    for b0 in range(0, nqk, 4):
        stream.add(comp_qk(b0), ld_fm(b0), "w")
    stream.add(comp_qc, ld_fm(nqk), "w")

    ntm = 6 if l == 0 else 8

    def ld_tm(b):
        def f(slot):
            k.dma("sp", slot.ap.rearrange("p a k c -> p (a k c)"), WT[b].rearrange("p k c -> p (k c)"), slot.res.ev, writes=[slot])
        return f

    def comp_tm(b):
        def f(slot):
            w = slot.ap.rearrange("p a k c -> p (a k c)").rearrange("p (k c) -> p k c", c=512)
            vs = st.vstage[st.vi % 2]
            st.vi += 1
            for sub in range(4):
                bk = c.bank[st.mbanks[st.oi % len(st.mbanks)]]
                st.oi += 1
                _mmg(k, [(bk[:, 0:512], xnT[:, kc, sub * 128:(sub + 1) * 128], w[:, kc, :], kc == 0, kc == KC - 1) for kc in range(KC)],
                     [xnT, slot], [bk])
                o = vs[:, sub, :]
                if l == 1 and b >= 4:
                    k.op("act", (lambda o=o, bk=bk: lambda h: h.activation(o, bk[:, 0:512], AF.Silu))(), [bk], [vs])
                else:
                    st.ei += 1
                    _copy(k, "act" if st.ei % 2 else "dve", o, bk[:, 0:512], [bk], [vs])
            if l == 1 and b >= 4:
                dst = c.SG[t * TT:(t + 1) * TT, (b - 4) * 512:(b - 3) * 512]
            else:
                dst = c.VS[t * TT:(t + 1) * TT, b * 512:(b + 1) * 512]
            k.dma("pool", dst.rearrange("(s p) c -> p s c", p=128), vs.ap, vs.res.ev, reads=[vs])
        return f

    for b in range(ntm):
        stream.add(comp_tm(b), ld_tm(b), "w")


def phase_p1_l0(c, si, S):
    k, ar = c.k, c.ar
    ar.reset()
    st = Ctx()
    alloc_common(c, st)
    st.tbanks = [0, 1]
    st.mbanks = [2, 3]
    st.cbanks = [4, 5, 6, 7]
    wpool = [ar.alloc(f"w{i}", [4, KC, 128], BF16, ev=k.new_ev("w")) for i in range(3)]
    xpool = [ar.alloc(f"x{i}", [4, D], F32, ev=k.new_ev("x")) for i in range(2)]
    xnT = [ar.alloc(f"xnT{i}", [KC, TT], BF16) for i in range(2)]
    stream = Stream()
    for t in range(S // TT):
        def ldx(slot, t=t):
            k.dma("sp", slot.ap, c.x_in[si][t * TT:(t + 1) * TT, :].rearrange("(s p) d -> p s d", p=128), slot.res.ev, writes=[slot])

        def cx(slot, t=t):
            norm_transpose(c, slot, 0, xnT[t % 2], st)
        stream.add(cx, ldx, "x")
        in_proj_units(c, stream, 0, t, S, xnT[t % 2], st)
    stream.run({"w": wpool, "x": xpool}, hook=lambda: bg_issue(c, 3))
    k.barrier()


def phase_p2_l0(c, si, S):
    k, ar = c.k, c.ar
    ar.reset()
    sc = float(HD ** -0.5)
    slots = []
    for i in range(2):
        sl = Ctx()
        sl.qn = ar.alloc(f"qn{i}", [S], BF16, ev=k.new_ev("qn"))
        sl.kn = ar.alloc(f"kn{i}", [S], BF16, ev=k.new_ev("kn"))
        sl.vp = ar.alloc(f"vp{i}", [S + 2048], BF16, ev=k.new_ev("vp"))
        sl.et = ar.alloc(f"et{i}", [8, 128], F32, ev=k.new_ev("et"))
        slots.append(sl)
    kds = [ar.alloc(f"kd{i}", [S + 2048], BF16) for i in range(2)]
    qds = [ar.alloc(f"qd{i}", [S], BF16) for i in range(2)]
    acc = ar.alloc("acc", [2, S], F32)
    pp = [ar.alloc(f"p{i}", [4, 128], F32) for i in range(4)]
    pm = [ar.alloc(f"pm{i}", [4, 128], BF16) for i in range(4)]
    ucnt = [0]
    mixo = ar.alloc("mixo", [S], BF16, ev=k.new_ev("mixo"))
    stream = Stream()
    cnt = [0]

    def mk_load(h_, g):
        win, dil = DIL[g]
        L = S // dil
        nt = L // 128
        blk = g * 8 + h_

        def f(sl):
            k.dma("sp", sl.qn.ap, c.QKT[blk][:, 0:S], sl.qn.res.ev, writes=[sl.qn])
            k.dma("sp", sl.kn.ap, c.QKT[24 + blk][:, 0:S], sl.kn.res.ev, writes=[sl.kn])
            k.dma("sp", sl.et.ap, c.etab[blk], sl.et.res.ev, writes=[sl.et])
            ev = sl.vp.res.ev
            vp4 = sl.vp[:, 0:dil * (nt + 1) * 128].rearrange("p (r j d) -> p r j d", r=dil, j=nt + 1)
            vcol = c.VS[:, blk * 128:(blk + 1) * 128]
            if nt >= 2:
                if dil == 1:
                    src = vcol[64: 64 + (nt - 1) * 128].rearrange("(j p) d -> p j d", p=128)
                    k.dma("sp", vp4[:, 0, 1:nt, :], src, ev, writes=[sl.vp])
                else:
                    for r in range(dil):
                        src = vcol[64 * dil: 64 * dil + (nt - 1) * 128 * dil].rearrange("(j p r) d -> p r j d", p=128, r=dil)[:, r]
                        k.dma("sp", vp4[:, r, 1:nt, :], src, ev, writes=[sl.vp])
            src = vcol[0:64 * dil].rearrange("(p r) d -> p r d", r=dil)
            k.dma("sp", vp4[64:128, :, 0, :], src, ev, writes=[sl.vp])
            src = vcol[(L - 64) * dil: L * dil].rearrange("(p r) d -> p r d", r=dil)
            k.dma("sp", vp4[0:64, :, nt, :], src, ev, writes=[sl.vp])
        return f

    def mk_comp(h_, g):
        win, dil = DIL[g]
        L = S // dil
        nt = L // 128

        def f(sl):
            kd = kds[ucnt[0] % 2]
            qd = qds[ucnt[0] % 2]
            ucnt[0] += 1
            kd3 = kd[:, 0:dil * (L + 128)].rearrange("p (r i) -> p r i", r=dil)
            vp4 = sl.vp[:, 0:dil * (nt + 1) * 128].rearrange("p (r j d) -> p r j d", r=dil, j=nt + 1)
            k.op("pool", lambda h: h.memset(vp4[0:64, :, 0, :], 0.0), [], [sl.vp])
            k.op("pool", lambda h: h.memset(vp4[64:128, :, nt, :], 0.0), [], [sl.vp])
            k.op("pool", lambda h: h.memset(kd3[:, :, 0:64], 0.0), [], [kd])
            k.op("pool", lambda h: h.memset(kd3[:, :, L + 64:L + 128], 0.0), [], [kd])
            k.op("pool", lambda h: h.tensor_copy(kd3[:, :, 64:64 + L], sl.kn.ap.rearrange("p (i r) -> p r i", r=dil)), [sl.kn], [kd])
            if dil > 1:
                q3 = qd.ap.rearrange("p (r i) -> p r i", r=dil)
                k.op("act", lambda h: h.activation(q3, sl.qn.ap.rearrange("p (i r) -> p r i", r=dil), AF.Copy), [sl.qn], [qd])
                qres = qd
            else:
                q3 = sl.qn.ap.rearrange("p (r i) -> p r i", r=1)
                qres = sl.qn
            acc4 = acc.ap.rearrange("p w (i r) -> p w r i", r=dil)
            tiles = [(r, m) for r in range(dil) for m in range(nt)]
            npairs = len(tiles) // 2
            state = {}

            def stage_a(pi):
                n = cnt[0]
                cnt[0] += 1
                bs = c.bank[n % 4]
                bn = c.bank[4 + n % 4]
                p_ = pp[n % 4]
                pm_ = pm[n % 4]
                pr = [tiles[2 * pi], tiles[2 * pi + 1]]
                items = []
                for ti, (r, m) in enumerate(pr):
                    for ab in range(2):
                        items.append((bs[:, (ti * 2 + ab) * 128:(ti * 2 + ab + 1) * 128],
                                      kd3[:, r, 128 * m + 128 * ab: 128 * m + 128 * ab + 128],
                                      q3[:, r, 128 * m:128 * m + 128], True, True))
                _mmg(k, items, [kd, qres], [bs])
                k.op("act", (lambda p_=p_, bs=bs: lambda h: h.activation(p_.ap.rearrange("p a b -> p (a b)"), bs[:, 0:512], AF.Exp, scale=sc))(), [bs], [p_])
                vis = [6 if nt == 1 else (0 if m == 0 else (4 if m == nt - 1 else 2)) for (r, m) in pr]
                if vis[0] == vis[1]:
                    vi = vis[0]
                    ebc = sl.et[:, vi:vi + 2, :].unsqueeze(1).to_broadcast([128, 2, 2, 128])
                    k.op("dve", (lambda p_=p_, pm_=pm_, ebc=ebc: lambda h: h.tensor_tensor(pm_.ap.rearrange("p (t a) b -> p t a b", a=2), p_.ap.rearrange("p (t a) b -> p t a b", a=2), ebc, ALU.mult))(),
                         [p_, sl.et], [pm_])
                else:
                    for ti, vi in enumerate(vis):
                        k.op("dve", (lambda p_=p_, pm_=pm_, ti=ti, vi=vi: lambda h: h.tensor_tensor(pm_[:, 2 * ti:2 * ti + 2, :], p_[:, 2 * ti:2 * ti + 2, :], sl.et[:, vi:vi + 2, :], ALU.mult))(),
                             [p_, sl.et], [pm_])

                state[pi] = (bn, pm_, pr)

            def stage_b(pi):
                bn, pm_, pr = state.pop(pi)
                items = []
                for ti, (r, m) in enumerate(pr):
                    for ab in range(2):
                        items.append((bn[:, ti * 128:(ti + 1) * 128], vp4[:, r, m + ab, :], pm_[:, 2 * ti + ab, :], ab == 0, ab == 1))
                    for ab in range(2):
                        items.append((bn[:, 256 + ti * 128:256 + (ti + 1) * 128], c.ones.ap, pm_[:, 2 * ti + ab, :], ab == 0, ab == 1))
                _mmg(k, items, [sl.vp, pm_, c.ones], [bn])
                (r0, m0) = pr[0]
                if nt >= 2:
                    dst = acc4[:, :, r0, 128 * m0:128 * m0 + 256]
                    src = bn[:, 0:512].rearrange("p (w b) -> p w b", w=2)
                else:
                    dst = acc4[:, :, r0:r0 + 2, 0:128]
                    src = bn[:, 0:512].rearrange("p (w a b) -> p w a b", w=2, a=2)
                if g == 0:
                    k.op("act", (lambda dst=dst, src=src: lambda h: h.activation(dst, src, AF.Copy))(), [bn], [acc])
                else:
                    k.op("dve", (lambda dst=dst, src=src: lambda h: h.tensor_tensor(dst, src, dst, ALU.add))(), [bn, acc], [acc])

            SK = 2
            for idx in range(npairs + SK):
                if idx < npairs:
                    stage_a(idx)
                if idx >= SK:
                    stage_b(idx - SK)
            if g == 2:
                k.op("act", lambda h: h.activation(acc[:, 1, :], acc[:, 1, :], AF.Ln), [acc], [acc])
                k.op("act", lambda h: h.activation(acc[:, 1, :], acc[:, 1, :], AF.Exp, scale=-1.0), [acc], [acc])
                k.op("pool", lambda h: h.tensor_tensor(mixo.ap, acc[:, 0, :], acc[:, 1, :], ALU.mult), [acc], [mixo])
                k.dma("pool", c.MIXT[h_][:, 0:S], mixo.ap, mixo.res.ev, reads=[mixo])
        return f

    for h_ in range(8):
        for g in range(3):
            stream.add(mk_comp(h_, g), mk_load(h_, g), "u")
    stream.run({"u": slots}, hook=lambda: bg_issue(c, 8))
    bg_issue(c, 100000)
    k.bg.clear()
    k.barrier()


def phase_fused(c, si, S, l):
    k, ar = c.k, c.ar
    ar.reset()
    st = Ctx()
    alloc_common(c, st)
    st.tbanks = [0, 1]
    st.mbanks = [2, 3, 4, 5]
    st.cbanks = [6, 7, 0, 1]
    nch = 12 if l == 0 else 20
    WO = c.AWO if l == 0 else c.BWO
    nparts = 1 if l == 0 else 2
    nchp = nch // nparts
    wpool = [ar.alloc(f"w{i}", [4, KC, 128], BF16, ev=k.new_ev("w")) for i in range(3)]
    xpool = [ar.alloc("x1", [4, D], F32, ev=k.new_ev("x1"))]
    mpool = [ar.alloc("mixT", [nch, TT], BF16, ev=k.new_ev("mixT"))]
    xnT = ar.alloc("xnT", [KC, TT], BF16)
    actT = ar.alloc("actT", [22, TT], BF16)
    sgt = [ar.alloc(f"sg{i}", [TT], F32) for i in range(2)]
    if l == 1:
        gbc = ar.alloc("gbc", [D], F32, ev=k.new_ev("gbc"))
        k.dma("sp", gbc.ap, c.gfinal.partition_broadcast(128), gbc.res.ev, writes=[gbc])
    xsrc = c.x_in[si] if l == 0 else c.X1
    stream = Stream()
    gi = [0]

    def flat(slot):
        return slot.ap.rearrange("p a k c -> p (a k c)")

    for t in range(S // TT):
        rows = slice(t * TT, (t + 1) * TT)
        cols = slice(t * TT, (t + 1) * TT)

        def ldx(slot, rows=rows):
            k.dma("sp", slot.ap, xsrc[rows, :].rearrange("(s p) d -> p s d", p=128), slot.res.ev, writes=[slot])

        def cx(slot):
            st.x1 = slot

        def ldm(slot, cols=cols):
            k.dma("sp", slot.ap, c.MIXT[0:nch, :, cols].rearrange("c p s -> p c s"), slot.res.ev, writes=[slot])

        def cm(slot):
            st.mixT = slot
        stream.add(cx, ldx, "x", release=[])
        stream.add(cm, ldm, "m", release=[])

        for cg in range(4):
            for part in range(nparts):
                def ldw(slot, cg=cg, part=part):
                    w = flat(slot)[:, 0:nchp * 512].rearrange("p (j c) -> p j c", c=512)
                    k.dma("sp", w, WO[part * nchp * 128:(part + 1) * nchp * 128, cg * 512:(cg + 1) * 512].rearrange("(j p) c -> p j c", p=128),
                          slot.res.ev, writes=[slot])

                def cw(slot, cg=cg, part=part):
                    w = flat(slot)[:, 0:nchp * 512].rearrange("p (j c) -> p j c", c=512)
                    x1, mixT = st.x1, st.mixT
                    for sub in range(4):
                        bk = c.bank[2 + sub]
                        _mmg(k, [(bk[:, 0:512], mixT[:, part * nchp + j, sub * 128:(sub + 1) * 128], w[:, j, :],
                                  part == 0 and j == 0, part == nparts - 1 and j == nchp - 1) for j in range(nchp)], [mixT, slot], [bk])
                        if part == nparts - 1:
                            o = x1[:, sub, cg * 512:(cg + 1) * 512]
                            k.op("dve", (lambda o=o, bk=bk: lambda h: h.tensor_tensor(o, bk[:, 0:512], o, ALU.add))(), [bk, x1], [x1])
                stream.add(cw, ldw, "w", release=["w"] + (["m"] if (cg == 3 and part == nparts - 1) else []))

        def cnorm(slot):
            norm_transpose(c, st.x1, 4 + l, xnT, st)
        stream.add(cnorm)
        for half in range(2):
            for jj in range(0, 22, 2):
                j0 = half * 22 + jj

                def ldg(slot, j0=j0):
                    k.dma("sp", slot[:, 0:2], c.GU[l][j0:j0 + 2].rearrange("b p k c -> p b k c"), slot.res.ev, writes=[slot])
                    k.dma("sp", slot[:, 2:4], c.GU[l][44 + j0:44 + j0 + 2].rearrange("b p k c -> p b k c"), slot.res.ev, writes=[slot])

                def cg_(slot, jj=jj):
                    for ci in range(2):
                        n = gi[0]
                        gi[0] += 1
                        bg = c.bank[[6, 0][n % 2]]
                        bu = c.bank[[7, 1][n % 2]]
                        sg = sgt[n % 2]
                        _mmg(k, [(bg[:, 0:TT], slot[:, ci, kc, :], xnT[:, kc, :], kc == 0, kc == KC - 1) for kc in range(KC)], [slot, xnT], [bg])
                        _mmg(k, [(bu[:, 0:TT], slot[:, 2 + ci, kc, :], xnT[:, kc, :], kc == 0, kc == KC - 1) for kc in range(KC)], [slot, xnT], [bu])
                        k.op("act", (lambda sg=sg, bg=bg: lambda h: h.activation(sg.ap, bg[:, 0:TT], AF.Silu))(), [bg], [sg])
                        o = actT[:, jj + ci, :]
                        k.op("dve", (lambda o=o, bu=bu, sg=sg: lambda h: h.tensor_tensor(o, bu[:, 0:TT], sg.ap, ALU.mult))(), [bu, sg], [actT])
                stream.add(cg_, ldg, "w")
            for cg in range(4):
                for part in range(2):
                    def ldd(slot, cg=cg, part=part, half=half):
                        w = flat(slot)[:, 0:11 * 512].rearrange("p (j c) -> p j c", c=512)
                        r0 = (half * 22 + part * 11) * 128
                        k.dma("sp", w, c.WD[l][r0:r0 + 11 * 128, cg * 512:(cg + 1) * 512].rearrange("(j p) c -> p j c", p=128), slot.res.ev, writes=[slot])

                    def cd(slot, cg=cg, part=part):
                        w = flat(slot)[:, 0:11 * 512].rearrange("p (j c) -> p j c", c=512)
                        x1 = st.x1
                        for sub in range(4):
                            bk = c.bank[2 + sub]
                            _mmg(k, [(bk[:, 0:512], actT[:, part * 11 + j, sub * 128:(sub + 1) * 128], w[:, j, :],
                                      part == 0 and j == 0, part == 1 and j == 10) for j in range(11)], [actT, slot], [bk])
                            if part == 1:
                                o = x1[:, sub, cg * 512:(cg + 1) * 512]
                                k.op("dve", (lambda o=o, bk=bk: lambda h: h.tensor_tensor(o, bk[:, 0:512], o, ALU.add))(), [bk, x1], [x1])
                    stream.add(cd, ldd, "w")

        if l == 0:
            def cstore(slot, rows=rows):
                x1 = st.x1
                k.dma("pool", c.X1[rows, :].rearrange("(s p) d -> p s d", p=128), x1.ap, x1.res.ev, reads=[x1])
                norm_transpose(c, x1, 1, xnT, st)
            stream.add(cstore, release=["x"])
            in_proj_units(c, stream, 1, t, S, xnT, st)
        else:
            def cfinal(slot, rows=rows):
                x1 = st.x1
                for sub in range(4):
                    ssc = st.ss[:, sub:sub + 1]
                    rsc = st.rstd[:, sub:sub + 1]
                    k.op("dve", (lambda ssc=ssc: lambda h: h.memset(ssc, 0.0))(), [], [st.ss])
                    k.op("act", (lambda ssc=ssc, sub=sub: lambda h: h.activation(st.junk.ap, x1[:, sub, :], AF.Square, accum_out=ssc))(), [x1, st.ss], [st.junk, st.ss])
                    k.op("act", (lambda ssc=ssc, rsc=rsc: lambda h: h.activation(rsc, ssc, AF.Ln, scale=1.0 / D, bias=EPS))(), [st.ss], [st.rstd])
                    k.op("act", (lambda rsc=rsc: lambda h: h.activation(rsc, rsc, AF.Exp, scale=-0.5))(), [st.rstd], [st.rstd])
                    k.op("dve", (lambda rsc=rsc, sub=sub: lambda h: h.scalar_tensor_tensor(x1[:, sub, :], x1[:, sub, :], rsc, gbc.ap, ALU.mult, ALU.mult))(),
                         [x1, st.rstd, gbc], [x1])
                k.dma("pool", c.y_out[si][rows, :].rearrange("(s p) d -> p s d", p=128), x1.ap, x1.res.ev, reads=[x1])
            stream.add(cfinal, release=["x"])
    stream.run({"w": wpool, "x": xpool, "m": mpool})
    k.barrier()


def phase_p2_l1(c, si, S):
    k, ar = c.k, c.ar
    ar.reset()
    n = S // 128
    rscale = float(256 ** -0.5)
    qT = ar.alloc("qT", [2, S], BF16, ev=k.new_ev("qT"))
    kT = ar.alloc("kT", [2, S], BF16, ev=k.new_ev("kT"))
    vt = ar.alloc("vt", [n, 256], BF16, ev=k.new_ev("vt"))
    sg = ar.alloc("sgt", [n, 256], BF16, ev=k.new_ev("sg"))
    oacc = ar.alloc("oacc", [n, 256], F32)
    yT = ar.alloc("yT", [2, S], BF16, ev=k.new_ev("yT"))
    DT = ar.alloc("DT", [128], F32)
    t1 = ar.alloc("t1", [128], F32)
    t2 = ar.alloc("t2", [128], F32)
    Sf = [ar.alloc(f"S{d}", [2, 256], F32) for d in range(2)]
    Sb16 = [[ar.alloc(f"Sb{d}{i}", [2, 256], BF16) for i in range(2)] for d in range(2)]
    pD = [ar.alloc(f"pD{i}", [128], BF16) for i in range(2)]
    kdk = [[ar.alloc(f"kdk{d}{i}", [256], BF16) for i in range(2)] for d in range(2)]
    ss = ar.alloc("ss", [n], F32)
    rstd = ar.alloc("rstd", [n], F32)
    junk = ar.alloc("junk", [256], BF16)
    yn = [ar.alloc(f"yn{i}", [256], BF16) for i in range(2)]
    for h_ in range(8):
        k.dma("sp", qT.ap, c.QKT[2 * h_:2 * h_ + 2, :, 0:S].rearrange("a p s -> p a s"), qT.res.ev, writes=[qT])
        k.dma("sp", kT.ap, c.QKT[16 + 2 * h_:16 + 2 * h_ + 2, :, 0:S].rearrange("a p s -> p a s"), kT.res.ev, writes=[kT])
        k.dma("sp", vt.ap, c.VS[0:S, h_ * 256:(h_ + 1) * 256].rearrange("(n p) c -> p n c", p=128), vt.res.ev, writes=[vt])
        k.dma("sp", sg.ap, c.SG[0:S, h_ * 256:(h_ + 1) * 256].rearrange("(n p) c -> p n c", p=128), sg.res.ev, writes=[sg])
        lgf = c.lg[:, h_:h_ + 1]
        lgb = c.lg[:, 8 + h_:9 + h_]
        k.op("act", (lambda lgf=lgf: lambda h: h.activation(t1.ap, c.rt[:, 0, :], AF.Exp, scale=lgf))(), [c.rt, c.lg], [t1])
        k.op("act", (lambda lgb=lgb: lambda h: h.activation(t2.ap, c.rt[:, 1, :], AF.Exp, scale=lgb))(), [c.rt, c.lg], [t2])
        k.op("dve", lambda h: h.tensor_tensor(t1.ap, t1.ap, c.rt[:, 2, :], ALU.mult), [t1, c.rt], [t1])
        k.op("dve", lambda h: h.tensor_tensor(t2.ap, t2.ap, c.rt[:, 3, :], ALU.mult), [t2, c.rt], [t2])
        k.op("dve", lambda h: h.tensor_tensor(DT.ap, t1.ap, t2.ap, ALU.add), [t1, t2], [DT])
        k.op("dve", lambda h: h.tensor_scalar(DT.ap, DT.ap, rscale, 0.0, ALU.mult, ALU.add), [DT], [DT])
        k.op("pool", lambda h: h.memset(oacc.ap, 0.0), [], [oacc])
        for d in range(2):
            k.op("pool", (lambda d=d: lambda h: h.memset(Sf[d].ap, 0.0))(), [], [Sf[d]])
            k.op("pool", (lambda d=d: lambda h: h.memset(Sb16[d][1].ap, 0.0))(), [], [Sb16[d][1]])

        def chunk_of(d, i):
            return i if d == 0 else n - 1 - i

        def emit_ktrans(d, i):
            ch = chunk_of(d, i)
            cs = slice(ch * 128, (ch + 1) * 128)
            b_t = c.bank[4 + d]
            kk = kdk[d][i % 2]
            kcol = c.kd[:, d, h_:h_ + 1]
            _mmg(k, [(b_t[:, a * 128:(a + 1) * 128], kT[:, a, cs], c.identb.ap, True, True) for a in range(2)], [kT, c.identb], [b_t])
            k.op("act", (lambda kk=kk, b_t=b_t, kcol=kcol: lambda h: h.activation(kk.ap, b_t[:, 0:256], AF.Copy, scale=kcol))(), [b_t, c.kd], [kk])

        for d in range(2):
            emit_ktrans(d, 0)
        for i in range(n):
            for d in range(2):
                ch = chunk_of(d, i)
                cs = slice(ch * 128, (ch + 1) * 128)
                qcol = c.kd[:, 2 + d, h_:h_ + 1]
                gcol_ = c.gC[:, d * 8 + h_: d * 8 + h_ + 1]
                b_o = c.bank[2 + d]
                b_s = c.bank[6 + d]
                kk = kdk[d][i % 2]
                s_prev = Sb16[d][(i + 1) % 2]
                s_next = Sb16[d][i % 2]
                o = oacc[:, ch, :]
                if i + 1 < n:
                    emit_ktrans(d, i + 1)
                if d == 0:
                    b_sc = c.bank[i % 2]
                    pd_ = pD[i % 2]
                    _mmg(k, [(b_sc[:, 0:128], kT[:, a, cs], qT[:, a, cs], a == 0, a == 1) for a in range(2)], [kT, qT], [b_sc])
                    k.op("dve", (lambda pd_=pd_, b_sc=b_sc: lambda h: h.tensor_tensor(pd_.ap, b_sc[:, 0:128], DT.ap, ALU.mult))(), [b_sc, DT], [pd_])
                _mmg(k, [(b_s[:, a * 256:(a + 1) * 256], kk[:, a * 128:(a + 1) * 128], vt[:, ch, :], True, True) for a in range(2)], [kk, vt], [b_s])
                _mmg(k, [(b_o[:, 0:256], qT[:, a, cs], s_prev[:, a, :], a == 0, a == 1) for a in range(2)], [qT, s_prev], [b_o])
                if d == 0:
                    _mmg(k, [(b_o[:, 256:512], pd_.ap, vt[:, ch, :], True, True)], [pd_, vt], [b_o])
                    k.op("dve", (lambda o=o, b_o=b_o: lambda h: h.tensor_tensor(o, b_o[:, 256:512], o, ALU.add))(), [b_o, oacc], [oacc])
                k.op("dve", (lambda o=o, b_o=b_o, qcol=qcol: lambda h: h.scalar_tensor_tensor(o, b_o[:, 0:256], qcol, o, ALU.mult, ALU.add))(), [b_o, oacc, c.kd], [oacc])
                sfl = Sf[d].ap.rearrange("p a b -> p (a b)")
                k.op("dve", (lambda sfl=sfl, b_s=b_s, gcol_=gcol_: lambda h: h.scalar_tensor_tensor(sfl, sfl, gcol_, b_s[:, 0:512], ALU.mult, ALU.add))(), [b_s, Sf[d], c.gC], [Sf[d]])
                k.op("act", (lambda d=d, s_next=s_next: lambda h: h.activation(s_next.ap, Sf[d].ap, AF.Copy))(), [Sf[d]], [s_next])
        k.op("dve", lambda h: h.memset(ss.ap, 0.0), [], [ss])
        for ch in range(n):
            ssc = ss[:, ch:ch + 1]
            o = oacc[:, ch, :]
            k.op("act", (lambda ssc=ssc, o=o: lambda h: h.activation(junk.ap, o, AF.Square, accum_out=ssc))(), [oacc, ss], [junk, ss])
        k.op("act", lambda h: h.activation(rstd.ap, ss.ap, AF.Ln, scale=1.0 / 256, bias=EPS), [ss], [rstd])
        k.op("act", lambda h: h.activation(rstd.ap, rstd.ap, AF.Exp, scale=-0.5), [rstd], [rstd])
        for ch in range(n):
            cs = slice(ch * 128, (ch + 1) * 128)
            rsc = rstd[:, ch:ch + 1]
            o = oacc[:, ch, :]
            yn_ = yn[ch % 2]
            b_y = c.bank[ch % 2]
            k.op("dve", (lambda yn_=yn_, o=o, rsc=rsc, ch=ch: lambda h: h.scalar_tensor_tensor(yn_.ap, o, rsc, sg[:, ch, :], ALU.mult, ALU.mult))(), [oacc, rstd, sg], [yn_])
            _mmg(k, [(b_y[:, a * 128:(a + 1) * 128], yn_[:, a * 128:(a + 1) * 128], c.identb.ap, True, True) for a in range(2)], [yn_, c.identb], [b_y])
            k.op("act", (lambda b_y=b_y, cs=cs: lambda h: h.activation(yT[:, :, cs], b_y[:, 0:256].rearrange("p (a b) -> p a b", b=128), AF.Copy))(), [b_y], [yT])
        k.dma("pool", c.MIXT[2 * h_:2 * h_ + 2, :, 0:S].rearrange("a p s -> p a s"), yT.ap, yT.res.ev, reads=[yT])
    k.barrier()


def run_sequence(c, si, S):
    import os
    stop = int(os.environ.get("KDBG_STOP", "99"))
    memkv(c, si)
    if stop < 1:
        return
    phase_p1_l0(c, si, S)
    if stop < 2:
        return
    phase_p2_l0(c, si, S)
    if stop < 3:
        return
    phase_fused(c, si, S, 0)
    if stop < 4:
        return
    phase_p2_l1(c, si, S)
    if stop < 5:
        return
    phase_fused(c, si, S, 1)


def _const_tables():
    a = np.arange(128, dtype=np.float64)[:, None]
    b = np.arange(128, dtype=np.float64)[None, :]
    etab = np.zeros((24, 128, 8, 128), np.float32)
    for g, (win, dil) in enumerate(DIL):
        for h_ in range(8):
            slope = 2.0 ** (-8.0 * (g * 8 + h_ + 1) / 24.0)
            A = np.where(a >= b, np.exp(-slope * dil * np.abs(a - 64 - b)), 0.0)
            B = np.where(a <= b, np.exp(-slope * dil * np.abs(a + 64 - b)), 0.0)
            Af = A * (a >= 64)
            Bl = B * (a < 64)
            for i, t in enumerate((Af, B, A, B, A, Bl, Af, Bl)):
                etab[g * 8 + h_, :, i, :] = t
    rtab = np.zeros((128, 4, 128), np.float32)
    rtab[:, 0, :] = np.maximum(b - a, 0)
    rtab[:, 1, :] = np.maximum(a - b, 0)
    rtab[:, 2, :] = (b >= a)
    rtab[:, 3, :] = (a > b)
    p = np.arange(128, dtype=np.float32)
    ccol = np.stack([127 - p, p, p + 1, 128 - p], axis=1).astype(np.float32)
    return etab, rtab, ccol


_NC_CACHE = {}


def kernel(x_prompt, x_sample, mem_prompt, mem_sample, norm_mix, norm_mem, w_mem_kv, a_w_in, a_w_out,
           b_w_in, b_w_out, b_decay_fwd, b_decay_bwd, norm_ffn, w_gate_up, w_down, norm_final):
    f = lambda a: np.ascontiguousarray(np.asarray(a, dtype=np.float32))
    S_list = (x_prompt.shape[1], x_sample.shape[1])
    if S_list not in _NC_CACHE:
        _NC_CACHE[S_list] = build(list(S_list))
    nc = _NC_CACHE[S_list]
    etab, rtab, ccol = _const_tables()

    def col(g):
        return f(g).reshape(16, 128).T

    gcols = np.ascontiguousarray(np.stack([col(norm_mix[0]), col(norm_mix[1]), col(norm_mem[0]), col(norm_mem[1]),
                                           col(norm_ffn[0]), col(norm_ffn[1]), col(norm_final)], axis=0))
    shared = {
        "gcols": gcols, "gfinal": f(norm_final).reshape(1, D),
        "decay": np.ascontiguousarray(np.concatenate([f(b_decay_fwd).reshape(-1), f(b_decay_bwd).reshape(-1)]).reshape(1, 16)),
        "ident": np.eye(128, dtype=np.float32), "etab": etab, "rtab": rtab, "ccol": ccol,
        "w_mem_kv": f(w_mem_kv), "a_w_in": f(a_w_in)[0], "a_w_out": f(a_w_out)[0], "b_w_in": f(b_w_in)[0],
        "b_w_out": f(b_w_out)[0], "w_gate_up": f(w_gate_up), "w_down": f(w_down),
    }
    n = x_prompt.shape[0]
    in_maps = []
    for i in range(n):
        m = dict(shared)
        m["x0"] = f(x_prompt[i])
        m["x1"] = f(x_sample[i])
        m["mem0"] = f(mem_prompt[i])
        m["mem1"] = f(mem_sample[i])
        in_maps.append(m)
    res = run_bass_kernel_spmd(nc, in_maps, core_ids=list(range(n)))
    y_p = np.stack([res.results[i]["y0"] for i in range(n)], axis=0).astype(np.float32)
    y_s = np.stack([res.results[i]["y1"] for i in range(n)], axis=0).astype(np.float32)
    return (y_p, y_s)
```
